# Optimizing a Trainium2 kernel written in Bass

```python
import math
import jax
import jax.numpy as jnp
from jax import lax
import numpy as np

D_MODEL = 2048
BATCH = 2
SEQ = 8192
DEPTH = 2

CHUNK = 64
Q_BLOCK = 128
EPS = 1e-6
CONV_WIDTH = 4

ML_HEADS = 4
ML_DV = D_MODEL // ML_HEADS
ML_DK = ML_DV // 2
ML_WIDTH = ML_HEADS * ML_DV

SSM_HEAD_DIM = 64
SSM_WIDTH = D_MODEL
SSM_HEADS = SSM_WIDTH // SSM_HEAD_DIM
SSM_GROUPS = 4
SSM_STATE = 128
SSM_CONV_CH = SSM_WIDTH + 2 * SSM_GROUPS * SSM_STATE

ATT_HEADS = D_MODEL // 128
ATT_HEAD_DIM = 128
ATT_WIDTH = ATT_HEADS * ATT_HEAD_DIM
KV_RANK = 512
IDX_HEADS = 16
IDX_DIM = 64
TOPK_MAX = 256
REL_BUCKETS = 32
REL_MAX_DIST = 128

GDN_QK_HEADS = D_MODEL // 128
GDN_V_HEADS = 2 * GDN_QK_HEADS
GDN_HEAD_DIM = 128
GDN_QK_WIDTH = GDN_QK_HEADS * GDN_HEAD_DIM
GDN_V_WIDTH = GDN_V_HEADS * GDN_HEAD_DIM
GDN_CONV_CH = 2 * GDN_QK_WIDTH + GDN_V_WIDTH

SPLIT_AB = (ML_HEADS * ML_DK, ML_HEADS * ML_DK, ML_WIDTH, ML_WIDTH, ML_WIDTH, ML_HEADS, ML_HEADS,
            SSM_WIDTH, SSM_CONV_CH, SSM_HEADS)
SPLIT_CD = (ATT_WIDTH, KV_RANK, ATT_WIDTH, IDX_HEADS * IDX_DIM, IDX_DIM, IDX_HEADS,
            GDN_CONV_CH, GDN_V_WIDTH, GDN_V_HEADS, GDN_V_HEADS)
IN_AB = sum(SPLIT_AB)
OUT_AB = ML_WIDTH + SSM_WIDTH
IN_CD = sum(SPLIT_CD)
OUT_CD = ATT_WIDTH + GDN_V_WIDTH

kernel_name = 'chunk_causal_hybrid_mlstm_ssd_dsa_gdn'

F32 = jnp.float32


def rms_norm(x, g):
    xf = x.astype(F32)
    y = xf * lax.rsqrt(jnp.mean(xf * xf, axis=-1, keepdims=True) + EPS)
    return (y * g.astype(F32)).astype(x.dtype)


def l2_norm(x):
    return x * lax.rsqrt(jnp.sum(x * x, axis=-1, keepdims=True) + EPS)


def split_cols(h, sizes):
    offs = np.cumsum(sizes)[:-1].tolist()
    return jnp.split(h, offs, axis=-1)


def causal_conv(x, w, b=None):
    k, c = w.shape
    y = lax.conv_general_dilated(x, w[:, None, :].astype(x.dtype), window_strides=(1,),
                                 padding=[(k - 1, 0)], dimension_numbers=('NWC', 'WIO', 'NWC'),
                                 feature_group_count=c)
    if b is not None:
        y = y + b.astype(y.dtype)
    return y


def to_chunks(a):
    b, t = a.shape[:2]
    return jnp.moveaxis(a.reshape(b, t // CHUNK, CHUNK, *a.shape[2:]), 1, 0)


def from_chunks(a):
    nc, b, l = a.shape[:3]
    return jnp.moveaxis(a, 0, 1).reshape(b, nc * l, *a.shape[3:])


def mlstm_scan(q, k, v, i_pre, log_f):
    bsz = q.shape[0]
    causal = jnp.tril(jnp.ones((CHUNK, CHUNK), bool))

    def step(carry, inp):
        c_st, n_st, m_st = carry
        qc, kc, vc, ic, fc = inp
        bcum = jnp.cumsum(fc, axis=1)
        d = bcum[:, :, None, :] - bcum[:, None, :, :] + ic[:, None, :, :]
        d = jnp.where(causal[None, :, :, None], d, -jnp.inf)
        inter = bcum + m_st[:, None, :]
        m_row = jnp.maximum(inter, jnp.max(d, axis=2))
        w_inter = jnp.exp(inter - m_row)
        p = jnp.exp(d - m_row[:, :, None, :]) * jnp.einsum('blhd,bshd->blsh', qc, kc)
        num = (w_inter[..., None] * jnp.einsum('blhd,bhde->blhe', qc, c_st)
               + jnp.einsum('blsh,bshe->blhe', p, vc))
        den = w_inter * jnp.einsum('blhd,bhd->blh', qc, n_st) + jnp.sum(p, axis=2)
        h = num / jnp.maximum(jnp.abs(den), jnp.exp(-m_row))[..., None]
        b_last = bcum[:, -1]
        g = b_last[:, None] - bcum + ic
        m_new = jnp.maximum(b_last + m_st, jnp.max(g, axis=1))
        w_old = jnp.exp(b_last + m_st - m_new)
        w_in = jnp.exp(g - m_new[:, None])
        c_new = w_old[..., None, None] * c_st + jnp.einsum('blh,blhd,blhe->bhde', w_in, kc, vc)
        n_new = w_old[..., None] * n_st + jnp.einsum('blh,blhd->bhd', w_in, kc)
        return (c_new, n_new, m_new), h

    init = (jnp.zeros((bsz, ML_HEADS, ML_DK, ML_DV), F32), jnp.zeros((bsz, ML_HEADS, ML_DK), F32),
            jnp.zeros((bsz, ML_HEADS), F32))
    _, h = lax.scan(step, init, (to_chunks(q), to_chunks(k), to_chunks(v), to_chunks(i_pre), to_chunks(log_f)))
    return from_chunks(h)


def mlstm_branch(q, k, v, o_pre, z, i_pre, f_pre, b_i, b_f, norm_g):
    bsz, t, _ = q.shape
    q = q.astype(F32).reshape(bsz, t, ML_HEADS, ML_DK)
    k = k.astype(F32).reshape(bsz, t, ML_HEADS, ML_DK) * (ML_DK ** -0.5)
    v = v.astype(F32).reshape(bsz, t, ML_HEADS, ML_DV)
    i_gate = i_pre.astype(F32) + b_i.astype(F32)
    log_f = jax.nn.log_sigmoid(f_pre.astype(F32) + b_f.astype(F32))
    h = mlstm_scan(q, k, v, i_gate, log_f)
    h = rms_norm(h, norm_g.reshape(ML_HEADS, ML_DV)).reshape(bsz, t, ML_WIDTH)
    return h * jax.nn.sigmoid(o_pre.astype(F32)) * jax.nn.silu(z.astype(F32))


def ssd_scan(x, dt, a, bm, cm):
    bsz = x.shape[0]
    hg = a.shape[1]
    causal = jnp.tril(jnp.ones((CHUNK, CHUNK), bool))

    def step(s, inp):
        xc, dtc, bc, cc = inp
        acum = jnp.cumsum(dtc * a, axis=1)
        seg = acum[:, :, None] - acum[:, None, :]
        decay = jnp.exp(jnp.where(causal[None, :, :, None, None], seg, -jnp.inf))
        cb = jnp.einsum('blgn,bsgn->blsg', cc, bc)
        xdt = xc * dtc[..., None]
        y = (jnp.einsum('blsg,blsgh,bsghp->blghp', cb, decay, xdt)
             + jnp.einsum('blgn,bghpn->blghp', cc, s) * jnp.exp(acum)[..., None])
        a_last = acum[:, -1]
        s_new = (jnp.exp(a_last)[..., None, None] * s
                 + jnp.einsum('bsgh,bsghp,bsgn->bghpn', jnp.exp(a_last[:, None] - acum), xdt, bc))
        return s_new, y

    s0 = jnp.zeros((bsz, SSM_GROUPS, hg, SSM_HEAD_DIM, SSM_STATE), F32)
    _, y = lax.scan(step, s0, (to_chunks(x), to_chunks(dt), to_chunks(bm), to_chunks(cm)))
    return from_chunks(y)


def mamba2_branch(z, xbc, dt_raw, conv_w, conv_b, dt_bias, a_log, d_skip, norm_g):
    bsz, t, _ = xbc.shape
    hg = SSM_HEADS // SSM_GROUPS
    xbc = jax.nn.silu(causal_conv(xbc, conv_w, conv_b)).astype(F32)
    xs, bm, cm = split_cols(xbc, (SSM_WIDTH, SSM_GROUPS * SSM_STATE, SSM_GROUPS * SSM_STATE))
    xs = xs.reshape(bsz, t, SSM_GROUPS, hg, SSM_HEAD_DIM)
    bm = bm.reshape(bsz, t, SSM_GROUPS, SSM_STATE)
    cm = cm.reshape(bsz, t, SSM_GROUPS, SSM_STATE)
    dt = jax.nn.softplus(dt_raw.astype(F32) + dt_bias.astype(F32)).reshape(bsz, t, SSM_GROUPS, hg)
    a = -jnp.exp(a_log.astype(F32)).reshape(SSM_GROUPS, hg)
    y = ssd_scan(xs, dt, a, bm, cm) + xs * d_skip.astype(F32).reshape(SSM_GROUPS, hg)[..., None]
    y = y.reshape(bsz, t, SSM_GROUPS, hg * SSM_HEAD_DIM)
    y = y * jax.nn.silu(z.astype(F32)).reshape(bsz, t, SSM_GROUPS, hg * SSM_HEAD_DIM)
    y = rms_norm(y, norm_g.reshape(SSM_GROUPS, hg * SSM_HEAD_DIM))
    return y.reshape(bsz, t, SSM_WIDTH)


def t5_bucket(rel):
    nb = REL_BUCKETS // 2
    max_exact = nb // 2
    ret = jnp.where(rel > 0, nb, 0)
    n = jnp.abs(rel)
    nf = jnp.maximum(n, max_exact).astype(F32)
    large = max_exact + (jnp.log(nf / max_exact) / math.log(REL_MAX_DIST / max_exact)
                         * (nb - max_exact)).astype(jnp.int32)
    large = jnp.minimum(large, nb - 1)
    return ret + jnp.where(n < max_exact, n, large)


def dsa_branch(q, ckv, z, iq, ik, iw, kv_norm_g, w_uk, w_uv, rel_bias):
    bsz, t, _ = q.shape
    top_k = min(TOPK_MAX, t // 4)
    q = q.astype(F32).reshape(bsz, t, ATT_HEADS, ATT_HEAD_DIM)
    ckv = rms_norm(ckv.astype(F32), kv_norm_g)
    iq = iq.astype(F32).reshape(bsz, t, IDX_HEADS, IDX_DIM)
    ik = ik.astype(F32)
    iw = iw.astype(F32) * (IDX_HEADS ** -0.5 * IDX_DIM ** -0.5)
    w_uk = w_uk.astype(F32)
    w_uv = w_uv.astype(F32)
    rel_bias = rel_bias.astype(F32)
    key_pos = jnp.arange(t)
    scale = ATT_HEAD_DIM ** -0.5

    def block(start):
        qpos = start + jnp.arange(Q_BLOCK)
        limit = (qpos // CHUNK + 1) * CHUNK
        admissible = key_pos[None, :] < limit[:, None]
        iq_b = lax.dynamic_slice_in_dim(iq, start, Q_BLOCK, axis=1)
        iw_b = lax.dynamic_slice_in_dim(iw, start, Q_BLOCK, axis=1)
        q_b = lax.dynamic_slice_in_dim(q, start, Q_BLOCK, axis=1)
        score = jnp.einsum('bqhd,bsd->bqhs', iq_b, ik)
        index = jnp.einsum('bqh,bqhs->bqs', iw_b, jax.nn.relu(score))
        index = jnp.where(admissible[None], index, -jnp.inf)
        _, sel = lax.top_k(index, top_k)
        valid = sel < limit[None, :, None]
        c_sel = jax.vmap(lambda c, i: c[i])(ckv, sel)
        q_abs = jnp.einsum('bqhd,rhd->bqhr', q_b, w_uk)
        logits = jnp.einsum('bqhr,bqkr->bqhk', q_abs, c_sel) * scale
        bias = rel_bias[t5_bucket(sel - qpos[None, :, None])]
        logits = logits + jnp.moveaxis(bias, -1, 2)
        logits = jnp.where(valid[:, :, None, :], logits, -jnp.inf)
        p = jax.nn.softmax(logits, axis=-1)
        o_lat = jnp.einsum('bqhk,bqkr->bqhr', p, c_sel)
        return jnp.einsum('bqhr,rhd->bqhd', o_lat, w_uv)

    starts = jnp.arange(t // Q_BLOCK) * Q_BLOCK
    o = lax.map(block, starts)
    o = jnp.moveaxis(o, 0, 1).reshape(bsz, t, ATT_WIDTH)
    return o * jax.nn.silu(z.astype(F32))


def gated_delta_scan(q, k, v, beta, g):
    bsz = q.shape[0]
    rep = GDN_V_HEADS // GDN_QK_HEADS
    causal = jnp.tril(jnp.ones((CHUNK, CHUNK), bool))
    strict = jnp.tril(jnp.ones((CHUNK, CHUNK), bool), k=-1)
    eye = jnp.eye(CHUNK, dtype=F32)

    def step(s, inp):
        qc, kc, vc, bc, gr = inp
        qc = jnp.repeat(qc, rep, axis=2)
        kc = jnp.repeat(kc, rep, axis=2)
        gc = jnp.cumsum(gr, axis=1)
        seg = gc[:, :, None] - gc[:, None, :]
        decay = jnp.exp(jnp.where(causal[None, :, :, None], seg, -jnp.inf))
        kk = jnp.einsum('blhd,bshd->blsh', kc, kc)
        a_mat = jnp.where(strict[None, :, :, None], bc[:, :, None, :] * kk * decay, 0.0)
        lhs = eye + jnp.moveaxis(a_mat, 3, 1)
        vb = jnp.moveaxis(vc * bc[..., None], 2, 1)
        kb = jnp.moveaxis(kc * (bc * jnp.exp(gc))[..., None], 2, 1)
        sol = lax.linalg.triangular_solve(lhs, jnp.concatenate([vb, kb], axis=-1),
                                          left_side=True, lower=True, unit_diagonal=True)
        u = sol[..., :GDN_HEAD_DIM] - jnp.einsum('bhlk,bhkv->bhlv', sol[..., GDN_HEAD_DIM:], s)
        qk = jnp.einsum('blhd,bshd->bhls', qc, kc) * jnp.moveaxis(decay, 3, 1)
        o = (jnp.einsum('blhd,bhdv->bhlv', qc, s) * jnp.moveaxis(jnp.exp(gc), 1, 2)[..., None]
             + jnp.einsum('bhls,bhsv->bhlv', qk, u))
        g_last = gc[:, -1]
        s_new = (jnp.exp(g_last)[..., None, None] * s
                 + jnp.einsum('bsh,bshd,bhsv->bhdv', jnp.exp(g_last[:, None] - gc), kc, u))
        return s_new, jnp.moveaxis(o, 1, 2)

    s0 = jnp.zeros((bsz, GDN_V_HEADS, GDN_HEAD_DIM, GDN_HEAD_DIM), F32)
    _, o = lax.scan(step, s0, (to_chunks(q), to_chunks(k), to_chunks(v), to_chunks(beta), to_chunks(g)))
    return from_chunks(o)


def gdn_branch(qkv, z, b_pre, a_pre, conv_w, dt_bias, a_log, norm_g):
    bsz, t, _ = qkv.shape
    qkv = jax.nn.silu(causal_conv(qkv, conv_w)).astype(F32)
    q, k, v = split_cols(qkv, (GDN_QK_WIDTH, GDN_QK_WIDTH, GDN_V_WIDTH))
    q = l2_norm(q.reshape(bsz, t, GDN_QK_HEADS, GDN_HEAD_DIM)) * (GDN_HEAD_DIM ** -0.5)
    k = l2_norm(k.reshape(bsz, t, GDN_QK_HEADS, GDN_HEAD_DIM))
    v = v.reshape(bsz, t, GDN_V_HEADS, GDN_HEAD_DIM)
    beta = jax.nn.sigmoid(b_pre.astype(F32))
    g = -jnp.exp(a_log.astype(F32)) * jax.nn.softplus(a_pre.astype(F32) + dt_bias.astype(F32))
    o = gated_delta_scan(q, k, v, beta, g)
    o = rms_norm(o, norm_g) * jax.nn.silu(z.astype(F32)).reshape(bsz, t, GDN_V_HEADS, GDN_HEAD_DIM)
    return o.reshape(bsz, t, GDN_V_WIDTH)


def layer_ab(x, norm_g, w_in, ml_b_i, ml_b_f, ml_norm, ssm_conv_w, ssm_conv_b, ssm_dt_bias,
             ssm_a_log, ssm_d, ssm_norm, w_out):
    h = rms_norm(x, norm_g)
    proj = jnp.einsum('btd,de->bte', h, w_in)
    q, k, v, o_pre, z_a, i_pre, f_pre, z_b, xbc, dt_raw = split_cols(proj, SPLIT_AB)
    y_a = mlstm_branch(q, k, v, o_pre, z_a, i_pre, f_pre, ml_b_i, ml_b_f, ml_norm)
    y_b = mamba2_branch(z_b, xbc, dt_raw, ssm_conv_w, ssm_conv_b, ssm_dt_bias, ssm_a_log, ssm_d, ssm_norm)
    y = jnp.concatenate([y_a, y_b], axis=-1).astype(x.dtype)
    return x + jnp.einsum('bte,ed->btd', y, w_out)


def layer_cd(x, norm_g, w_in, kv_norm, w_uk, w_uv, gdn_conv_w, gdn_dt_bias, gdn_a_log, gdn_norm,
             w_out, rel_bias):
    h = rms_norm(x, norm_g)
    proj = jnp.einsum('btd,de->bte', h, w_in)
    q_c, ckv, z_c, iq, ik, iw, qkv, z_d, b_pre, a_pre = split_cols(proj, SPLIT_CD)
    y_c = dsa_branch(q_c, ckv, z_c, iq, ik, iw, kv_norm, w_uk, w_uv, rel_bias)
    y_d = gdn_branch(qkv, z_d, b_pre, a_pre, gdn_conv_w, gdn_dt_bias, gdn_a_log, gdn_norm)
    y = jnp.concatenate([y_c, y_d], axis=-1).astype(x.dtype)
    return x + jnp.einsum('bte,ed->btd', y, w_out)


def setup_inputs(seed: int = 0) -> dict:
    key = jax.random.key(seed)
    keys = iter(jax.random.split(key, 40))
    n_ab = (DEPTH + 1) // 2
    n_cd = DEPTH // 2

    def normal(shape, scale):
        return jax.random.normal(next(keys), shape, F32) * scale

    def unif(shape, lo, hi):
        return jax.random.uniform(next(keys), shape, F32, lo, hi)

    def gain(shape):
        return 1.0 + normal(shape, 0.02)

    def dt_bias(shape):
        dt = jnp.exp(unif(shape, math.log(1e-3), math.log(1e-1)))
        return dt + jnp.log(-jnp.expm1(-dt))

    return {
        'x': normal((BATCH, SEQ, D_MODEL), 1.0),
        'rel_bias': normal((REL_BUCKETS, ATT_HEADS), 0.1),
        'ab_norm': gain((n_ab, D_MODEL)),
        'ab_w_in': normal((n_ab, D_MODEL, IN_AB), D_MODEL ** -0.5),
        'ab_ml_b_i': normal((n_ab, ML_HEADS), 0.1),
        'ab_ml_b_f': unif((n_ab, ML_HEADS), 3.0, 6.0),
        'ab_ml_norm': gain((n_ab, ML_WIDTH)),
        'ab_ssm_conv_w': normal((n_ab, CONV_WIDTH, SSM_CONV_CH), CONV_WIDTH ** -0.5),
        'ab_ssm_conv_b': normal((n_ab, SSM_CONV_CH), 0.02),
        'ab_ssm_dt_bias': dt_bias((n_ab, SSM_HEADS)),
        'ab_ssm_a_log': jnp.log(unif((n_ab, SSM_HEADS), 1.0, 16.0)),
        'ab_ssm_d': gain((n_ab, SSM_HEADS)),
        'ab_ssm_norm': gain((n_ab, SSM_WIDTH)),
        'ab_w_out': normal((n_ab, OUT_AB, D_MODEL), OUT_AB ** -0.5),
        'cd_norm': gain((n_cd, D_MODEL)),
        'cd_w_in': normal((n_cd, D_MODEL, IN_CD), D_MODEL ** -0.5),
        'cd_kv_norm': gain((n_cd, KV_RANK)),
        'cd_w_uk': normal((n_cd, KV_RANK, ATT_HEADS, ATT_HEAD_DIM), KV_RANK ** -0.5),
        'cd_w_uv': normal((n_cd, KV_RANK, ATT_HEADS, ATT_HEAD_DIM), KV_RANK ** -0.5),
        'cd_gdn_conv_w': normal((n_cd, CONV_WIDTH, GDN_CONV_CH), CONV_WIDTH ** -0.5),
        'cd_gdn_dt_bias': dt_bias((n_cd, GDN_V_HEADS)),
        'cd_gdn_a_log': jnp.log(unif((n_cd, GDN_V_HEADS), 1.0, 16.0)),
        'cd_gdn_norm': gain((n_cd, GDN_HEAD_DIM)),
        'cd_w_out': normal((n_cd, OUT_CD, D_MODEL), OUT_CD ** -0.5),
        'final_norm': gain((D_MODEL,)),
    }


def reference(x, rel_bias, ab_norm, ab_w_in, ab_ml_b_i, ab_ml_b_f, ab_ml_norm, ab_ssm_conv_w,
              ab_ssm_conv_b, ab_ssm_dt_bias, ab_ssm_a_log, ab_ssm_d, ab_ssm_norm, ab_w_out,
              cd_norm, cd_w_in, cd_kv_norm, cd_w_uk, cd_w_uv, cd_gdn_conv_w, cd_gdn_dt_bias,
              cd_gdn_a_log, cd_gdn_norm, cd_w_out, final_norm):
    for layer in range(DEPTH):
        i = layer // 2
        if layer % 2 == 0:
            x = layer_ab(x, ab_norm[i], ab_w_in[i], ab_ml_b_i[i], ab_ml_b_f[i], ab_ml_norm[i],
                         ab_ssm_conv_w[i], ab_ssm_conv_b[i], ab_ssm_dt_bias[i], ab_ssm_a_log[i],
                         ab_ssm_d[i], ab_ssm_norm[i], ab_w_out[i])
        else:
            x = layer_cd(x, cd_norm[i], cd_w_in[i], cd_kv_norm[i], cd_w_uk[i], cd_w_uv[i],
                         cd_gdn_conv_w[i], cd_gdn_dt_bias[i], cd_gdn_a_log[i], cd_gdn_norm[i],
                         cd_w_out[i], rel_bias)
    return rms_norm(x, final_norm)
```

```python
import numpy as np
from contextlib import ExitStack
import concourse.bass as bass
import concourse.mybir as mybir
from concourse.bass_utils import run_bass_kernel_spmd


F32 = mybir.dt.float32
BF16 = mybir.dt.bfloat16
AF = mybir.ActivationFunctionType
ALU = mybir.AluOpType
AX = mybir.AxisListType


class Res:
    __slots__ = ("name", "w", "r", "dsem", "dcnt", "persist")

    def __init__(self, name, persist=False):
        self.name = name
        self.persist = persist
        self.w = {}
        self.r = {}
        self.dsem = None
        self.dcnt = 0


class Prog:
    ENGS = ("sync", "act", "pe", "dve", "pool")

    def __init__(self, nc, es):
        self.nc = nc
        self.es = es
        self.q = {e: [] for e in self.ENGS}
        self.csem = {e: es.enter_context(nc.semaphore("c_" + e)) for e in ("act", "pe", "dve", "pool")}
        self.ccnt = {e: 0 for e in self.csem}
        self.seen = {e: {} for e in self.ENGS}
        self.final = []
        self.nsem = 4
        self.pending = {e: [] for e in self.ENGS}
        self.dres = []
        self.free_sems = []

    def res(self, name, persist=False):
        return Res(name, persist)

    def _waits(self, eng, reads, writes):
        toks = []
        for r in reads:
            toks.extend(r.w.values())
        for w in writes:
            toks.extend(w.w.values())
            toks.extend(w.r.values())
        if self.pending[eng]:
            toks.extend(self.pending[eng])
            self.pending[eng] = []
        need = {}
        seen = self.seen[eng]
        for (sem, val) in toks:
            k = id(sem)
            if seen.get(k, 0) >= val:
                continue
            if k not in need or need[k][1] < val:
                need[k] = (sem, val)
        for k, (sem, val) in need.items():
            seen[k] = val
        return list(need.values())

    def _mark(self, tok, reads, writes):
        k = id(tok[0])
        for r in reads:
            r.r[k] = tok
        for w in writes:
            w.w[k] = tok
            w.r = {}

    def op(self, eng, fn, reads=(), writes=()):
        waits = self._waits(eng, reads, writes)
        self.ccnt[eng] += 1
        tok = (self.csem[eng], self.ccnt[eng])
        self.q[eng].append((fn, waits, (self.csem[eng], 1)))
        self._mark(tok, reads, writes)

    def dma(self, eng, fn, reads=(), writes=(), tok_res=None, final=False):
        waits = self._waits(eng, reads, writes)
        tr = tok_res or (writes[0] if writes else reads[0])
        if tr.dsem is None:
            if self.free_sems:
                tr.dsem, tr.dcnt = self.free_sems.pop()
            else:
                tr.dsem = self.es.enter_context(self.nc.semaphore("d%d_%s" % (self.nsem, tr.name)))
                self.nsem += 1
            self.dres.append(tr)
        tr.dcnt += 16
        tok = (tr.dsem, tr.dcnt)
        self.q[eng].append((fn, waits, (tr.dsem, 16)))
        self._mark(tok, reads, writes)
        if final:
            self.final.append(tok)

    def barrier(self):
        toks = [(self.csem[e], self.ccnt[e]) for e in self.csem if self.ccnt[e] > 0]
        toks += [(r.dsem, r.dcnt) for r in self.dres]
        for e in self.ENGS:
            self.pending[e].extend(toks)

    def act(self, out, in_, func, reads=(), writes=(), **kw):
        self.op("act", lambda e: e.activation(out=out, in_=in_, func=func, **kw), reads, writes)

    def mm(self, out, lhsT, rhs, start, stop, reads=(), writes=(), **kw):
        self.op("pe", lambda e: e.matmul(out, lhsT=lhsT, rhs=rhs, start=start, stop=stop, **kw), reads, writes)

    def tr(self, out, in_, ident, reads=(), writes=()):
        self.op("pe", lambda e: e.transpose(out=out, in_=in_, identity=ident), reads, writes)

    def tt(self, out, in0, in1, op, reads=(), writes=(), eng="dve"):
        self.op(eng, lambda e: e.tensor_tensor(out=out, in0=in0, in1=in1, op=op), reads, writes)

    def ts(self, out, in0, s1, op0, s2=None, op1=None, reads=(), writes=(), eng="dve", **kw):
        if op1 is None:
            self.op(eng, lambda e: e.tensor_scalar(out=out, in0=in0, scalar1=s1, scalar2=None, op0=op0, **kw), reads, writes)
        else:
            self.op(eng, lambda e: e.tensor_scalar(out=out, in0=in0, scalar1=s1, scalar2=s2, op0=op0, op1=op1, **kw), reads, writes)

    def stt(self, out, in0, scalar, in1, op0, op1, reads=(), writes=()):
        self.op("dve", lambda e: e.scalar_tensor_tensor(out=out, in0=in0, scalar=scalar, in1=in1, op0=op0, op1=op1), reads, writes)

    def cp(self, out, in_, reads=(), writes=(), eng="dve"):
        if eng == "act":
            self.op("act", lambda e: e.copy(out=out, in_=in_), reads, writes)
        else:
            self.op(eng, lambda e: e.tensor_copy(out=out, in_=in_), reads, writes)

    def ld(self, out, in_, reads=(), writes=(), eng="sync", **kw):
        self.dma(eng, lambda e: e.dma_start(out=out, in_=in_, **kw), reads, writes)

    def st(self, out, in_, reads=(), writes=(), eng="sync", final=False, **kw):
        self.dma(eng, lambda e: e.dma_start(out=out, in_=in_, **kw), reads, writes, final=final)

    def emit(self, final_eng="sync", last=True):
        nc = self.nc
        fin = {}
        for (sem, val) in self.final:
            k = id(sem)
            if k not in fin or fin[k][1] < val:
                fin[k] = (sem, val)
        q = self.q
        engmap = {"sync": "sync", "act": "scalar", "pe": "tensor", "dve": "vector", "pool": "gpsimd"}

        def body_for(e):
            def body(engine):
                for (fn, waits, inc) in q[e]:
                    for (sem, val) in waits:
                        engine.wait_ge(sem, val)
                    ins = fn(engine)
                    ins.then_inc(inc[0], inc[1])
                if e == final_eng and last:
                    for (sem, val) in fin.values():
                        engine.wait_ge(sem, val)
            return body

        with nc.Block() as block:
            for e in self.ENGS:
                if q[e] or (e == final_eng and last):
                    getattr(block, engmap[e])(body_for(e))
        self.q = {e: [] for e in self.ENGS}
        if not last:
            keep = []
            for r in self.dres:
                if r.persist:
                    keep.append(r)
                else:
                    self.free_sems.append((r.dsem, r.dcnt))
                    r.dsem = None
            self.dres = keep


EPS = 1e-6
_uq = [0]


def UQ(n):
    _uq[0] += 1
    return f"{n}_u{_uq[0]}"
NF = 1290
NTM = 2048


def build_ab(T):
    nc = bass.Bass("TRN2", target_bir_lowering=False)
    KC = 16
    TT = 512
    NTI = T // TT
    NB = T // 128
    din = lambda n, s, d=F32: nc.dram_tensor(n, s, d, kind="ExternalInput").ap()
    xT = din("xT", [2048, T]); g2 = din("g2", [128, 16])
    Wf = din("Wf", [128, 16, NF]); Wt = din("Wt", [128, 16, NTM])
    bif = din("bif", [1, 2])
    gain_a_d = din("gain_a", [128, 512]); gain_b_d = din("gain_b", [128, 512])
    convw_d = din("convw", [128, 6, 4]); convb_d = din("convb", [128, 6])
    dtp_d = din("dtp", [8, 2]); dskip_d = din("dskip", [128, 8])
    ident_d = din("ident", [128, 128]); mask_d = din("maskneg", [128, 128])
    ya = nc.dram_tensor("ya", [T, 512], BF16, kind="ExternalOutput").ap()
    yb = nc.dram_tensor("yb", [T, 512], BF16, kind="ExternalOutput").ap()
    sc = lambda n, s, d: nc.dram_tensor(n, s, d).ap()
    qT_s = sc("qT_s", [256, T], BF16); kT_s = sc("kT_s", [256, T], BF16)
    xbc_s = sc("xbc_s", [768, T], F32); dt_s = sc("dt_s", [8, T], F32); if_s = sc("if_s", [2, T], F32)
    v_s = sc("v_s", [T, 512], BF16); G2_s = sc("G2_s", [T, 512], F32); Zb_s = sc("Zb_s", [T, 512], F32)
    rows_s = sc("rows_s", [1, T], F32); cols_s = sc("cols_s", [128, 2 * NB], F32)
    G_h = nc.dram_tensor("G_s", [8, T], F32); G_s = G_h.ap(); xsD_s = sc("xsD_s", [T, 512], F32)

    with ExitStack() as es0:
        P = Prog(nc, es0)
        R = P.res

        with ExitStack() as es:
            sb = lambda n, s, d: es.enter_context(nc.sbuf_tensor(UQ(n), s, d))
            ps = lambda n, s, d: es.enter_context(nc.psum_tensor(UQ(n), s, d))
            g_sb = sb("g_sb", [128, 16], F32); r_g = R("g")
            ones = sb("ones", [128, 128], BF16); r_ones = R("ones")
            wfb = sb("wfb", [128, KC, NF], BF16); wtb = sb("wtb", [128, KC, NTM], BF16)
            r_wd = R("wd"); r_wp = R("wp")
            wst = [sb(f"wst{i}", [128, KC, 128], F32) for i in range(2)]; r_wst = [R(f"wst{i}") for i in range(2)]
            xin = sb("xin", [128, KC, TT], F32); r_xin = R("xin")
            hT = sb("hT", [128, KC, TT], BF16); r_hT = R("hT")
            sq = [sb(f"sq{i}", [128, TT], BF16) for i in range(2)]; r_sq = [R(f"sq{i}") for i in range(2)]
            rstd = sb("rstd", [128, TT], F32); r_rstd = R("rstd")
            ssq = ps("ssq", [128, TT], F32); r_ssq = R("ssq")
            pf = [ps(f"pf{i}", [128, 512], F32) for i in range(3)]; r_pf = [R(f"pf{i}") for i in range(3)]
            stf = [sb(f"stf{i}", [128, 512], F32) for i in range(3)]; r_stf = [R(f"stf{i}") for i in range(3)]
            stb = [sb(f"stb{i}", [128, 512], BF16) for i in range(3)]; r_stb = [R(f"stb{i}") for i in range(3)]
            s1 = sb("s1", [128, 512], F32); r_s1 = R("s1")

            P.ld(g_sb[:], g2, writes=[r_g])
            P.op("pool", lambda e: e.memset(ones[:], 1.0), writes=[r_ones])
            chunks = [(Wf, wfb, c0, min(c0 + 128, NF)) for c0 in range(0, NF, 128)] + \
                     [(Wt, wtb, c0, c0 + 128) for c0 in range(0, NTM, 128)]
            for idx, (src, dst, c0, c1) in enumerate(chunks):
                b = idx % 2
                n = c1 - c0
                P.ld(wst[b][:, :, 0:n], src[:, :, c0:c1], writes=[r_wst[b]], eng="sync" if b == 0 else "act")
                if b == 0:
                    P.cp(dst[:, :, c0:c1], wst[b][:, :, 0:n], reads=[r_wst[b]], writes=[r_wd], eng="dve")
                else:
                    P.cp(dst[:, :, c0:c1], wst[b][:, :, 0:n], reads=[r_wst[b]], writes=[r_wp], eng="pool")
            rW = [r_wd, r_wp]
            xTv = xT.rearrange("(c p) t -> p c t", p=128)
            fm = [(0, 128, qT_s[0:128], BF16, 1.0), (128, 128, qT_s[128:256], BF16, 1.0),
                  (256, 128, kT_s[0:128], BF16, 0.0625), (384, 128, kT_s[128:256], BF16, 0.0625)]
            for j in range(6):
                fm.append((512 + j * 128, 128, xbc_s[j * 128:(j + 1) * 128], F32, 1.0))
            fm.append((1280, 8, dt_s[0:8], F32, 1.0))
            fm.append((1288, 1, if_s[0:1], F32, 1.0))
            fm.append((1289, 1, if_s[1:2], F32, 1.0))
            k = 0
            for ti in range(NTI):
                tsl = slice(ti * TT, (ti + 1) * TT)
                P.ld(xin[:], xTv[:, :, tsl], writes=[r_xin])
                for c in range(KC):
                    s = c % 2
                    P.act(sq[s][:], xin[:, c, :], AF.Square, reads=[r_xin], writes=[r_sq[s]])
                    P.mm(ssq[:], ones[:], sq[s][:], c == 0, c == KC - 1, reads=[r_ones, r_sq[s]], writes=[r_ssq])
                P.act(rstd[:], ssq[:], AF.Sqrt, bias=EPS, scale=1.0 / 2048, reads=[r_ssq], writes=[r_rstd])
                P.op("dve", lambda e: e.reciprocal(out=rstd[:], in_=rstd[:]), reads=[r_rstd], writes=[r_rstd])
                for c in range(KC):
                    P.stt(hT[:, c, :], xin[:, c, :], g_sb[:, c:c + 1], rstd[:], ALU.mult, ALU.mult,
                          reads=[r_xin, r_g, r_rstd], writes=[r_hT])
                for (c0, m, dst, dty, scale) in fm:
                    a = k % 3
                    k += 1
                    for c in range(KC):
                        P.mm(pf[a][0:m, :], wfb[:, c, c0:c0 + m], hT[:, c, :], c == 0, c == KC - 1,
                             reads=rW + [r_hT], writes=[r_pf[a]])
                    if dty == BF16:
                        P.act(stb[a][0:m, :], pf[a][0:m, :], AF.Copy, scale=scale, reads=[r_pf[a]], writes=[r_stb[a]])
                        P.st(dst[:, tsl], stb[a][0:m, :], reads=[r_stb[a]])
                    else:
                        P.act(stf[a][0:m, :], pf[a][0:m, :], AF.Copy, scale=scale, reads=[r_pf[a]], writes=[r_stf[a]])
                        P.st(dst[:, tsl], stf[a][0:m, :], reads=[r_stf[a]])
                for tb in range(TT // 128):
                    tok = slice(tb * 128, (tb + 1) * 128)
                    gtok = slice(ti * TT + tb * 128, ti * TT + (tb + 1) * 128)
                    for grp in range(4):
                        a = k % 3
                        k += 1
                        for c in range(KC):
                            P.mm(pf[a][:], hT[:, c, tok], wtb[:, c, grp * 512:(grp + 1) * 512], c == 0, c == KC - 1,
                                 reads=rW + [r_hT], writes=[r_pf[a]])
                        if grp == 0:
                            P.act(stb[a][:], pf[a][:], AF.Copy, reads=[r_pf[a]], writes=[r_stb[a]])
                            P.st(v_s[gtok, :], stb[a][:], reads=[r_stb[a]])
                        elif grp == 1:
                            P.act(s1[:], pf[a][:], AF.Sigmoid, reads=[r_pf[a]], writes=[r_s1])
                        elif grp == 2:
                            P.act(stf[a][:], pf[a][:], AF.Silu, reads=[r_pf[a]], writes=[r_stf[a]])
                            P.tt(stf[a][:], stf[a][:], s1[:], ALU.mult, reads=[r_s1, r_stf[a]], writes=[r_stf[a]])
                            P.st(G2_s[gtok, :], stf[a][:], reads=[r_stf[a]])
                        else:
                            P.act(stf[a][:], pf[a][:], AF.Silu, reads=[r_pf[a]], writes=[r_stf[a]])
                            P.st(Zb_s[gtok, :], stf[a][:], reads=[r_stf[a]])
            P.barrier()
            P.emit(last=False)

        with ExitStack() as es:
            sb = lambda n, s, d: es.enter_context(nc.sbuf_tensor(UQ(n), s, d))
            ps = lambda n, s, d: es.enter_context(nc.psum_tensor(UQ(n), s, d))
            rf = sb("rf", [1, T], F32); ri = sb("ri", [1, T], F32); r3 = sb("r3", [1, T], F32)
            r_rf = R("rf"); r_ri = R("ri"); r_r3 = R("r3")
            bif_sb = sb("bif_sb", [1, 2], F32); r_bif = R("bif")
            nbf = sb("nbf", [1, 1], F32); r_nbf = R("nbf")
            ident = sb("ident2a", [128, 128], F32); r_id = R("id2a")
            pcol = ps("pcol", [128, 2 * NB], F32); r_pcol = R("pcol")
            colsb = sb("colsb", [128, 2 * NB], F32); r_colsb = R("colsb")
            P.ld(rf[:], if_s[1:2, :], writes=[r_rf]); P.ld(ri[:], if_s[0:1, :], writes=[r_ri])
            P.ld(bif_sb[:], bif, writes=[r_bif]); P.ld(ident[:], ident_d, writes=[r_id])
            P.ts(nbf[:], bif_sb[:, 1:2], -1.0, ALU.mult, reads=[r_bif], writes=[r_nbf])
            P.act(rf[:], rf[:], AF.Exp, scale=-1.0, bias=nbf[:, 0:1], reads=[r_rf, r_nbf], writes=[r_rf])
            P.act(rf[:], rf[:], AF.Ln, bias=1.0, reads=[r_rf], writes=[r_rf])
            P.op("dve", lambda e: e.tensor_tensor_scan(out=r3[:], data0=rf[:], data1=rf[:], initial=0.0,
                                                       op0=ALU.add, op1=ALU.max), reads=[r_rf], writes=[r_r3])
            P.stt(ri[:], ri[:], bif_sb[:, 0:1], r3[:], ALU.add, ALU.add, reads=[r_ri, r_bif, r_r3], writes=[r_ri])
            P.op("dve", lambda e: e.tensor_tensor_scan(out=rf[:], data0=ri[:], data1=ri[:], initial=0.0,
                                                       op0=ALU.max, op1=ALU.max), reads=[r_ri], writes=[r_rf])
            P.tt(r3[:], r3[:], rf[:], ALU.subtract, reads=[r_r3, r_rf], writes=[r_r3])
            P.ts(rf[:], rf[:], -1.0, ALU.mult, reads=[r_rf], writes=[r_rf])
            P.st(rows_s[0:1, :], rf[:], reads=[r_rf])
            for j in range(NB):
                P.tr(pcol[:, j:j + 1], ri[0:1, j * 128:(j + 1) * 128], ident[0:1, 0:1], reads=[r_ri, r_id], writes=[r_pcol])
                P.tr(pcol[:, NB + j:NB + j + 1], r3[0:1, j * 128:(j + 1) * 128], ident[0:1, 0:1], reads=[r_r3, r_id], writes=[r_pcol])
            P.cp(colsb[:], pcol[:], reads=[r_pcol], writes=[r_colsb])
            P.st(cols_s, colsb[:], reads=[r_colsb])
            P.barrier()
            P.emit(last=False)

        with ExitStack() as es:
            sb = lambda n, s, d: es.enter_context(nc.sbuf_tensor(UQ(n), s, d))
            ps = lambda n, s, d: es.enter_context(nc.psum_tensor(UQ(n), s, d))
            kT = sb("kT", [128, 2, T], BF16); r_kT = R("kT")
            v = sb("v", [128, NB, 512], BF16); r_v = R("v")
            nMxb = sb("nMxb", [128, T], F32); r_nMxb = R("nMxb")
            cols = sb("cols", [128, 2 * NB], F32); r_cols = R("cols")
            enm = sb("enm", [128, NB], F32); r_enm = R("enm")
            maskneg = sb("maskneg", [128, 128], F32); r_mask = R("mask")
            gain_a = sb("gain_a", [128, 512], F32); r_ga = R("ga")
            ones1 = sb("ones1", [128, 2], BF16); r_o1 = R("o1")
            dm = sb("dm", [128, 128], F32); r_dm = R("dm")
            qblk = [sb(f"qblk{i}", [128, 2, 128], BF16) for i in range(2)]; r_q = [R(f"q{i}") for i in range(2)]
            g2b = [sb(f"g2b{i}", [128, 512], F32) for i in range(2)]; r_g2b = [R(f"g2b{i}") for i in range(2)]
            sT = [ps(f"sT{i}", [128, 128], F32) for i in range(2)]; r_sT = [R(f"sT{i}") for i in range(2)]
            E = [sb(f"E{i}", [128, 128], F32) for i in range(2)]; r_E = [R(f"E{i}") for i in range(2)]
            pT = [sb(f"pT{i}", [128, 128], BF16) for i in range(2)]; r_pT = [R(f"pT{i}") for i in range(2)]
            acc = [ps(f"acc{i}", [128, 512], F32) for i in range(2)]; r_acc = [R(f"acc{i}") for i in range(2)]
            accd = [ps(f"accd{i}", [128, 2], F32) for i in range(2)]; r_accd = [R(f"accd{i}") for i in range(2)]
            junk = sb("junk", [128, 512], BF16); r_junk = R("junk")
            ss = sb("ss", [128, 1], F32); r_ss = R("ss")
            dd = sb("dd", [128, 1], F32); r_dd = R("dd")
            t1 = sb("t1", [128, 1], F32); r_t1 = R("t1")
            g3 = sb("g3", [128, 512], F32); r_g3 = R("g3")
            yout = [sb(f"yout{i}", [128, 512], BF16) for i in range(2)]; r_yout = [R(f"yout{i}") for i in range(2)]

            for j in range(2):
                P.ld(kT[:, j, :], kT_s[j * 128:(j + 1) * 128, :], writes=[r_kT], eng="sync")
            vv = v_s.rearrange("(j p) n -> p j n", p=128)
            for j0 in range(0, NB, 8):
                j1 = min(NB, j0 + 8)
                P.ld(v[:, j0:j1, :], vv[:, j0:j1, :], writes=[r_v], eng="act")
            P.ld(nMxb[:], rows_s[0:1, :].partition_broadcast(128), writes=[r_nMxb], eng="pool")
            P.ld(cols[:], cols_s, writes=[r_cols]); P.ld(maskneg[:], mask_d, writes=[r_mask]); P.ld(gain_a[:], gain_a_d, writes=[r_ga])
            P.op("pool", lambda e: e.memset(ones1[:], 1.0), writes=[r_o1])
            P.act(enm[:], cols[:, NB:2 * NB], AF.Exp, reads=[r_cols], writes=[r_enm])
            qv = qT_s.rearrange("(j p) t -> p j t", p=128)
            cnt = 0
            for tb in range(NB):
                b = tb % 2
                tsl = slice(tb * 128, (tb + 1) * 128)
                P.ld(qblk[b][:], qv[:, :, tsl], writes=[r_q[b]], eng="sync")
                P.ld(g2b[b][:], G2_s[tsl, :], writes=[r_g2b[b]], eng="sync")
                P.tt(dm[:], nMxb[:, tsl], maskneg[:], ALU.add, reads=[r_nMxb, r_mask], writes=[r_dm])
                for sbi in range(tb + 1):
                    a = cnt % 2
                    cnt += 1
                    ssl = slice(sbi * 128, (sbi + 1) * 128)
                    for j in range(2):
                        P.mm(sT[a][:], kT[:, j, ssl], qblk[b][:, j, :], j == 0, j == 1, reads=[r_kT, r_q[b]], writes=[r_sT[a]])
                    if sbi == tb:
                        P.act(E[a][:], dm[:], AF.Exp, bias=cols[:, sbi:sbi + 1], reads=[r_dm, r_cols], writes=[r_E[a]])
                    else:
                        P.act(E[a][:], nMxb[:, tsl], AF.Exp, bias=cols[:, sbi:sbi + 1], reads=[r_nMxb, r_cols], writes=[r_E[a]])
                    P.tt(pT[a][:], E[a][:], sT[a][:], ALU.mult, reads=[r_E[a], r_sT[a]], writes=[r_pT[a]])
                    P.mm(acc[b][:], pT[a][:], v[:, sbi, :], sbi == 0, sbi == tb, reads=[r_pT[a], r_v], writes=[r_acc[b]])
                    P.mm(accd[b][:], pT[a][:], ones1[:], sbi == 0, sbi == tb, reads=[r_pT[a], r_o1], writes=[r_accd[b]])
                P.act(junk[:], acc[b][:], AF.Square, accum_out=ss[:, 0:1], reads=[r_acc[b]], writes=[r_junk, r_ss])
                P.act(dd[:], accd[b][:, 0:1], AF.Abs, reads=[r_accd[b]], writes=[r_dd])
                P.tt(dd[:], dd[:], enm[:, tb:tb + 1], ALU.max, reads=[r_dd, r_enm], writes=[r_dd])
                P.op("dve", lambda e: e.reciprocal(out=dd[:], in_=dd[:]), reads=[r_dd], writes=[r_dd])
                P.tt(t1[:], dd[:], dd[:], ALU.mult, reads=[r_dd], writes=[r_t1])
                P.tt(t1[:], t1[:], ss[:], ALU.mult, reads=[r_t1, r_ss], writes=[r_t1])
                P.act(t1[:], t1[:], AF.Sqrt, bias=EPS, scale=1.0 / 512, reads=[r_t1], writes=[r_t1])
                P.op("dve", lambda e: e.reciprocal(out=t1[:], in_=t1[:]), reads=[r_t1], writes=[r_t1])
                P.tt(t1[:], t1[:], dd[:], ALU.mult, reads=[r_t1, r_dd], writes=[r_t1])
                P.tt(g3[:], g2b[b][:], gain_a[:], ALU.mult, reads=[r_g2b[b], r_ga], writes=[r_g3], eng="pool")
                P.stt(yout[b][:], acc[b][:], t1[:, 0:1], g3[:], ALU.mult, ALU.mult, reads=[r_acc[b], r_t1, r_g3], writes=[r_yout[b]])
                P.st(ya[tsl, :], yout[b][:], reads=[r_yout[b]], final=True)
            P.barrier()
            P.emit(last=False)

        with ExitStack() as es:
            sb = lambda n, s, d: es.enter_context(nc.sbuf_tensor(UQ(n), s, d))
            ps = lambda n, s, d: es.enter_context(nc.psum_tensor(UQ(n), s, d))
            xdt = sb("xdt", [128, NB, 512], BF16); r_xdt = R("xdt")
            bmT = sb("bmT", [128, T], BF16); r_bmT = R("bmT")
            cmT = sb("cmT", [128, T], BF16); r_cmT = R("cmT")
            dt_tm = sb("dt_tm", [128, NB, 8], F32); r_dttm = R("dttm")
            nG_tm = sb("nG_tm", [128, NB, 8], F32); r_nG = R("nG")
            ident = sb("ident3", [128, 128], F32); r_id = R("id3", True)
            maskneg = sb("maskneg3", [128, 128], F32); r_mask = R("mask3", True)
            gain_b = sb("gain_b", [128, 512], F32); r_gb = R("gb", True)
            dskip = sb("dskip", [128, 8], F32); r_dsk = R("dsk", True)
            P.ld(ident[:], ident_d, writes=[r_id]); P.ld(maskneg[:], mask_d, writes=[r_mask])
            P.ld(gain_b[:], gain_b_d, writes=[r_gb]); P.ld(dskip[:], dskip_d, writes=[r_dsk])
            with ExitStack() as es2:
                sb2 = lambda n, s, d: es2.enter_context(nc.sbuf_tensor(UQ(n), s, d))
                ps2 = lambda n, s, d: es2.enter_context(nc.psum_tensor(UQ(n), s, d))
                dtr = sb2("dtr", [8, T], F32); r_dtr = R("dtr")
                dAr = sb2("dAr", [8, T], F32); r_dAr = R("dAr")
                Gr = sb2("Gr", [8, T], F32); r_Gr = R("Gr")
                dtp = sb2("dtp", [8, 2], F32); r_dtp = R("dtp")
                acol = sb2("acol", [8, 1], F32); r_acol = R("acol")
                pdt = ps2("pdt", [128, NB * 8], F32); r_pdt = R("pdt")
                pG = ps2("pG", [128, NB * 8], F32); r_pG = R("pG")
                P.ld(dtr[:], dt_s, writes=[r_dtr]); P.ld(dtp[:], dtp_d, writes=[r_dtp])
                P.act(acol[:], dtp[:, 1:2], AF.Exp, reads=[r_dtp], writes=[r_acol])
                P.ts(acol[:], acol[:], -1.0, ALU.mult, reads=[r_acol], writes=[r_acol])
                P.act(dtr[:], dtr[:], AF.Exp, bias=dtp[:, 0:1], reads=[r_dtr, r_dtp], writes=[r_dtr])
                P.act(dtr[:], dtr[:], AF.Ln, bias=1.0, reads=[r_dtr], writes=[r_dtr])
                P.ts(dAr[:], dtr[:], acol[:, 0:1], ALU.mult, reads=[r_dtr, r_acol], writes=[r_dAr])
                P.op("dve", lambda e: e.tensor_tensor_scan(out=Gr[:], data0=dAr[:], data1=dAr[:], initial=0.0,
                                                           op0=ALU.add, op1=ALU.min), reads=[r_dAr], writes=[r_Gr])
                P.st(G_s, Gr[:], reads=[r_Gr])
                for j in range(NB):
                    P.tr(pdt[:, j * 8:(j + 1) * 8], dtr[0:8, j * 128:(j + 1) * 128], ident[0:8, 0:8], reads=[r_dtr, r_id], writes=[r_pdt])
                    P.tr(pG[:, j * 8:(j + 1) * 8], Gr[0:8, j * 128:(j + 1) * 128], ident[0:8, 0:8], reads=[r_Gr, r_id], writes=[r_pG])
                P.cp(dt_tm[:].rearrange("p j h -> p (j h)"), pdt[:], reads=[r_pdt], writes=[r_dttm])
                P.ts(nG_tm[:].rearrange("p j h -> p (j h)"), pG[:], -1.0, ALU.mult, reads=[r_pG], writes=[r_nG])
                P.barrier()
                P.emit(last=False)
            with ExitStack() as es2:
                sb2 = lambda n, s, d: es2.enter_context(nc.sbuf_tensor(UQ(n), s, d))
                ps2 = lambda n, s, d: es2.enter_context(nc.psum_tensor(UQ(n), s, d))
                convw = sb2("convw", [128, 6, 4], F32); r_cw = R("cw")
                convb = sb2("convb", [128, 6], F32); r_cb = R("cb")
                pre = [sb2(f"pre{i}", [128, 3 + TT], F32) for i in range(2)]; r_pre = [R(f"pre{i}") for i in range(2)]
                cacc = [sb2(f"cacc{i}", [128, TT], F32) for i in range(2)]; r_cacc = [R(f"cacc{i}") for i in range(2)]
                xsc = [sb2(f"xsc{i}", [128, TT], F32) for i in range(4)]; r_xsc = [R(f"xsc{i}") for i in range(4)]
                pxs = [ps2(f"pxs{i}", [128, 512], F32) for i in range(2)]; r_pxs = [R(f"pxs{i}") for i in range(2)]
                xsd = [sb2(f"xsd{i}", [128, 512], F32) for i in range(2)]; r_xsd = [R(f"xsd{i}") for i in range(2)]
                P.ld(convw[:], convw_d, writes=[r_cw]); P.ld(convb[:], convb_d, writes=[r_cb])
                k = 0
                kk = 0
                for ti in range(NTI):
                    for blk in range(6):
                        a = k % 2
                        k += 1
                        rows = slice(blk * 128, (blk + 1) * 128)
                        if ti == 0:
                            P.op("pool", lambda e, a=a: e.memset(pre[a][:, 0:3], 0.0), writes=[r_pre[a]])
                            P.ld(pre[a][:, 3:3 + TT], xbc_s[rows, 0:TT], writes=[r_pre[a]], eng="sync")
                        else:
                            P.ld(pre[a][:], xbc_s[rows, ti * TT - 3:(ti + 1) * TT], writes=[r_pre[a]], eng="sync")
                        P.ts(cacc[a][:], pre[a][:, 0:TT], convw[:, blk, 0:1], ALU.mult, reads=[r_pre[a], r_cw], writes=[r_cacc[a]])
                        for tap in range(1, 4):
                            P.stt(cacc[a][:], pre[a][:, tap:tap + TT], convw[:, blk, tap:tap + 1], cacc[a][:], ALU.mult, ALU.add,
                                  reads=[r_pre[a], r_cw, r_cacc[a]], writes=[r_cacc[a]])
                        tsl = slice(ti * TT, (ti + 1) * TT)
                        if blk < 4:
                            P.act(xsc[blk][:], cacc[a][:], AF.Silu, bias=convb[:, blk:blk + 1], reads=[r_cacc[a], r_cb], writes=[r_xsc[blk]])
                        elif blk == 4:
                            P.act(bmT[:, tsl], cacc[a][:], AF.Silu, bias=convb[:, blk:blk + 1], reads=[r_cacc[a], r_cb], writes=[r_bmT])
                        else:
                            P.act(cmT[:, tsl], cacc[a][:], AF.Silu, bias=convb[:, blk:blk + 1], reads=[r_cacc[a], r_cb], writes=[r_cmT])
                    for q in range(TT // 128):
                        j = ti * (TT // 128) + q
                        pa = kk % 2
                        kk += 1
                        for blk in range(4):
                            P.tr(pxs[pa][:, blk * 128:(blk + 1) * 128], xsc[blk][:, q * 128:(q + 1) * 128], ident[:],
                                 reads=[r_xsc[blk], r_id], writes=[r_pxs[pa]])
                        P.tt(xdt[:, j, :].rearrange("p (h d) -> p h d", h=8), pxs[pa][:].rearrange("p (h d) -> p h d", h=8),
                             dt_tm[:, j, :].unsqueeze(2).to_broadcast([128, 8, 64]), ALU.mult,
                             reads=[r_pxs[pa], r_dttm], writes=[r_xdt])
                        P.tt(xsd[pa][:].rearrange("p (h d) -> p h d", h=8), pxs[pa][:].rearrange("p (h d) -> p h d", h=8),
                             dskip[:].unsqueeze(2).to_broadcast([128, 8, 64]), ALU.mult,
                             reads=[r_pxs[pa], r_dsk], writes=[r_xsd[pa]])
                        P.st(xsD_s[j * 128:(j + 1) * 128, :], xsd[pa][:], reads=[r_xsd[pa]])
                P.barrier()
                P.emit(last=False)
            with ExitStack() as es2:
                sb2 = lambda n, s, d: es2.enter_context(nc.sbuf_tensor(UQ(n), s, d))
                ps2 = lambda n, s, d: es2.enter_context(nc.psum_tensor(UQ(n), s, d))
                Gb = [sb2(f"Gb{i}", [128, 8, 128], F32) for i in range(2)]; r_Gb = [R(f"Gb{i}") for i in range(2)]
                GbM = sb2("GbM", [128, 8, 128], F32); r_GbM = R("GbM")
                cbT = [ps2(f"cbT{i}", [128, 128], F32) for i in range(2)]; r_cbT = [R(f"cbT{i}") for i in range(2)]
                dec = [sb2(f"dec{i}", [128, 8, 128], F32) for i in range(2)]; r_dec = [R(f"dec{i}") for i in range(2)]
                Wt_ = [sb2(f"Wt{i}", [128, 8, 128], BF16) for i in range(2)]; r_Wt = [R(f"Wt{i}") for i in range(2)]
                yacc = [ps2(f"yacc{i}", [128, 512], F32) for i in range(2)]; r_yacc = [R(f"yacc{i}") for i in range(2)]
                xsdb = [sb2(f"xsdb{i}", [128, 512], F32) for i in range(2)]; r_xsdb = [R(f"xsdb{i}") for i in range(2)]
                zbb = [sb2(f"zbb{i}", [128, 512], F32) for i in range(2)]; r_zbb = [R(f"zbb{i}") for i in range(2)]
                tq = sb2("tq", [128, 512], F32); r_tq = R("tq")
                junk = sb2("junk3", [128, 512], BF16); r_junk = R("junk3")
                ss = sb2("ss3", [128, 1], F32); r_ss = R("ss3")
                yo = [sb2(f"yo{i}", [128, 512], BF16) for i in range(2)]; r_yo = [R(f"yo{i}") for i in range(2)]
                cnt = 0
                for tb in range(NB):
                    b = tb % 2
                    tsl = slice(tb * 128, (tb + 1) * 128)
                    gsrc = bass.AP(G_h, tb * 128, [[0, 128], [T, 8], [1, 128]])
                    P.ld(Gb[b][:], gsrc, writes=[r_Gb[b]], eng="pool")
                    P.ld(xsdb[b][:], xsD_s[tsl, :], writes=[r_xsdb[b]], eng="sync")
                    P.ld(zbb[b][:], Zb_s[tsl, :], writes=[r_zbb[b]], eng="sync")
                    P.tt(GbM[:], Gb[b][:], maskneg[:].unsqueeze(1).to_broadcast([128, 8, 128]), ALU.add,
                         reads=[r_Gb[b], r_mask], writes=[r_GbM])
                    for sbi in range(tb + 1):
                        a = cnt % 2
                        cnt += 1
                        ssl = slice(sbi * 128, (sbi + 1) * 128)
                        P.mm(cbT[a][:], bmT[:, ssl], cmT[:, tsl], True, True, reads=[r_bmT, r_cmT], writes=[r_cbT[a]])
                        for h in range(8):
                            if sbi == tb:
                                P.act(dec[a][:, h, :], GbM[:, h, :], AF.Exp, bias=nG_tm[:, sbi, h:h + 1], reads=[r_GbM, r_nG], writes=[r_dec[a]])
                            else:
                                P.act(dec[a][:, h, :], Gb[b][:, h, :], AF.Exp, bias=nG_tm[:, sbi, h:h + 1], reads=[r_Gb[b], r_nG], writes=[r_dec[a]])
                        P.tt(Wt_[a][:], dec[a][:], cbT[a][:].unsqueeze(1).to_broadcast([128, 8, 128]), ALU.mult,
                             reads=[r_dec[a], r_cbT[a]], writes=[r_Wt[a]])
                        for h in range(8):
                            P.mm(yacc[b][:, h * 64:(h + 1) * 64], Wt_[a][:, h, :], xdt[:, sbi, h * 64:(h + 1) * 64],
                                 (sbi == 0 and h == 0), sbi == tb, reads=[r_Wt[a], r_xdt], writes=[r_yacc[b]], skip_group_check=True)
                    P.tt(tq[:], yacc[b][:], xsdb[b][:], ALU.add, reads=[r_yacc[b], r_xsdb[b]], writes=[r_tq])
                    P.tt(tq[:], tq[:], zbb[b][:], ALU.mult, reads=[r_tq, r_zbb[b]], writes=[r_tq])
                    P.act(junk[:], tq[:], AF.Square, accum_out=ss[:, 0:1], reads=[r_tq], writes=[r_junk, r_ss])
                    P.act(ss[:], ss[:], AF.Sqrt, bias=EPS, scale=1.0 / 512, reads=[r_ss], writes=[r_ss])
                    P.op("dve", lambda e: e.reciprocal(out=ss[:], in_=ss[:]), reads=[r_ss], writes=[r_ss])
                    P.stt(yo[b][:], tq[:], ss[:, 0:1], gain_b[:], ALU.mult, ALU.mult, reads=[r_tq, r_ss, r_gb], writes=[r_yo[b]])
                    P.st(yb[tsl, :], yo[b][:], reads=[r_yo[b]], final=True)
                P.emit(last=True)
        print("nsem", P.nsem, "counts", P.ccnt)
    return nc


def ab_inputs(inp, b, g, T):
    f32 = np.float32
    W = inp["ab_w_in"][0]
    def cols(off, n):
        return W[:, off:off + n]
    q = cols(0 + 256 * g, 256); k = cols(1024 + 256 * g, 256); v = cols(2048 + 512 * g, 512)
    o = cols(4096 + 512 * g, 512); za = cols(6144 + 512 * g, 512)
    ip = cols(8192 + g, 1); fp = cols(8196 + g, 1); zb = cols(8200 + 512 * g, 512)
    xs = cols(10248 + 512 * g, 512); bm = cols(12296 + 128 * g, 128); cm = cols(12808 + 128 * g, 128)
    dt = cols(13320 + 8 * g, 8)
    Wf = np.concatenate([q, k, xs, bm, cm, dt, ip, fp], axis=1)
    Wt = np.concatenate([v, o, za, zb], axis=1)
    lay = lambda A: np.ascontiguousarray(A.reshape(16, 128, A.shape[1]).transpose(1, 0, 2))
    cw = inp["ab_ssm_conv_w"][0]; cb = inp["ab_ssm_conv_b"][0]
    ch = np.concatenate([np.arange(512 * g, 512 * g + 512), 2048 + np.arange(128 * g, 128 * g + 128),
                         2560 + np.arange(128 * g, 128 * g + 128)])
    convw = np.ascontiguousarray(cw[:, ch].T.reshape(6, 128, 4).transpose(1, 0, 2))
    convb = np.ascontiguousarray(cb[ch].reshape(6, 128).T)
    hs = slice(8 * g, 8 * g + 8)
    d = {
        "xT": np.ascontiguousarray(inp["x"][b, :T].T),
        "g2": np.ascontiguousarray(inp["ab_norm"][0].reshape(16, 128).T),
        "Wf": lay(Wf), "Wt": lay(Wt),
        "bif": np.array([[inp["ab_ml_b_i"][0][g], inp["ab_ml_b_f"][0][g]]], f32),
        "gain_a": np.ascontiguousarray(np.broadcast_to(inp["ab_ml_norm"][0][512 * g:512 * g + 512], (128, 512))),
        "gain_b": np.ascontiguousarray(np.broadcast_to(inp["ab_ssm_norm"][0][512 * g:512 * g + 512], (128, 512))),
        "convw": convw, "convb": convb,
        "dtp": np.ascontiguousarray(np.stack([inp["ab_ssm_dt_bias"][0][hs], inp["ab_ssm_a_log"][0][hs]], axis=1)),
        "dskip": np.ascontiguousarray(np.broadcast_to(inp["ab_ssm_d"][0][hs], (128, 8))),
        "ident": np.eye(128, dtype=f32),
        "maskneg": np.where(np.arange(128)[:, None] <= np.arange(128)[None, :], 0.0, -1e30).astype(f32),
    }
    return {k_: np.asarray(v_, f32) for k_, v_ in d.items()}


EPS = 1e-6


def build_out(K, Tt, final):
    nc = bass.Bass("TRN2", target_bir_lowering=False)
    KC = K // 128
    TT = min(512, Tt)
    NTB = TT // 128
    din = lambda n, s, d=F32: nc.dram_tensor(n, s, d, kind="ExternalInput").ap()
    yT = din("yT", [K, Tt], BF16)
    W = din("W", [128, KC, 2048])
    x = din("x", [Tt, 2048])
    if final:
        gfin_d = din("gfin", [128, 2048])
    out = nc.dram_tensor("out", [Tt, 2048], F32, kind="ExternalOutput").ap()
    with ExitStack() as es:
        P = Prog(nc, es)
        R = P.res
        sb = lambda n, s, d: es.enter_context(nc.sbuf_tensor(UQ(n), s, d))
        ps = lambda n, s, d: es.enter_context(nc.psum_tensor(UQ(n), s, d))
        yb = sb("yb", [128, KC, TT], BF16); r_yb = R("yb")
        wst = [sb(f"wst{i}", [128, 2, 512], F32) for i in range(2)]; r_wst = [R(f"wst{i}") for i in range(2)]
        wb = [sb(f"wb{i}", [128, KC, 512], BF16) for i in range(2)]
        r_wbd = [R(f"wbd{i}") for i in range(2)]; r_wbp = [R(f"wbp{i}") for i in range(2)]
        acc = [ps(f"acc{i}", [128, 512], F32) for i in range(4)]; r_acc = [R(f"acc{i}") for i in range(4)]
        xt = [sb(f"xt{i}", [128, 512], F32) for i in range(4)]; r_xt = [R(f"xt{i}") for i in range(4)]
        if final:
            xrow = sb("xrow", [128, NTB, 2048], F32); r_xrow = [R(f"xrow{i}") for i in range(NTB)]
            gfin = sb("gfin_sb", [128, 2048], F32); r_gf = R("gf")
            junk = sb("junk", [128, 2048], BF16); r_junk = R("junk")
            ss = sb("ss", [128, 1], F32); r_ss = R("ss")
            P.ld(gfin[:], gfin_d, writes=[r_gf])
        yv = yT.rearrange("(c p) t -> p c t", p=128)
        k = 0
        wk = 0
        for ti in range(Tt // TT):
            tsl = slice(ti * TT, (ti + 1) * TT)
            h = KC // 2
            P.ld(yb[:, 0:h, :], yv[:, 0:h, tsl], writes=[r_yb], eng="sync")
            P.ld(yb[:, h:KC, :], yv[:, h:KC, tsl], writes=[r_yb], eng="act")
            for nb in range(4):
                nsl = slice(nb * 512, (nb + 1) * 512)
                b = (ti * 4 + nb) % 2
                for c0 in range(0, KC, 2):
                    s = wk % 2
                    wk += 1
                    P.ld(wst[s][:], W[:, c0:c0 + 2, nsl], writes=[r_wst[s]], eng="sync" if s == 0 else "pool")
                    if s == 0:
                        P.cp(wb[b][:, c0:c0 + 2, :], wst[s][:], reads=[r_wst[s]], writes=[r_wbd[b]], eng="dve")
                    else:
                        P.cp(wb[b][:, c0:c0 + 2, :], wst[s][:], reads=[r_wst[s]], writes=[r_wbp[b]], eng="pool")
                for tb in range(NTB):
                    a = k % 4
                    k += 1
                    tok = slice(tb * 128, (tb + 1) * 128)
                    gtok = slice(ti * TT + tb * 128, ti * TT + (tb + 1) * 128)
                    P.ld(xt[a][:], x[gtok, nsl], writes=[r_xt[a]], eng="act")
                    for c in range(KC):
                        P.mm(acc[a][:], yb[:, c, tok], wb[b][:, c, :], c == 0, c == KC - 1,
                             reads=[r_yb, r_wbd[b], r_wbp[b]], writes=[r_acc[a]])
                    if final:
                        P.tt(xrow[:, tb, nsl], acc[a][:], xt[a][:], ALU.add, reads=[r_acc[a], r_xt[a]], writes=[r_xrow[tb]])
                    else:
                        P.tt(xt[a][:], acc[a][:], xt[a][:], ALU.add, reads=[r_acc[a], r_xt[a]], writes=[r_xt[a]])
                        P.st(out[gtok, nsl], xt[a][:], reads=[r_xt[a]], final=True)
            if final:
                for tb in range(NTB):
                    gtok = slice(ti * TT + tb * 128, ti * TT + (tb + 1) * 128)
                    P.act(junk[:], xrow[:, tb, :], AF.Square, accum_out=ss[:, 0:1], reads=[r_xrow[tb]], writes=[r_junk, r_ss])
                    P.act(ss[:], ss[:], AF.Sqrt, bias=EPS, scale=1.0 / 2048, reads=[r_ss], writes=[r_ss])
                    P.op("dve", lambda e: e.reciprocal(out=ss[:], in_=ss[:]), reads=[r_ss], writes=[r_ss])
                    P.stt(xrow[:, tb, :], xrow[:, tb, :], ss[:, 0:1], gfin[:], ALU.mult, ALU.mult,
                          reads=[r_xrow[tb], r_ss, r_gf], writes=[r_xrow[tb]])
                    P.st(out[gtok, :], xrow[:, tb, :], reads=[r_xrow[tb]], final=True)
        P.emit(last=True)
    return nc


def out_inputs(y_bf16, W, x, c, Tt, gfin=None):
    K = W.shape[0]
    d = {
        "yT": np.ascontiguousarray(y_bf16[c * Tt:(c + 1) * Tt].T),
        "W": np.ascontiguousarray(W.reshape(K // 128, 128, 2048).transpose(1, 0, 2)),
        "x": np.ascontiguousarray(x[c * Tt:(c + 1) * Tt]),
    }
    if gfin is not None:
        d["gfin"] = np.ascontiguousarray(np.broadcast_to(gfin, (128, 2048))).astype(np.float32)
    return d


EPS = 1e-6
BIG = 30000.0


def inproj_pass(nc, P, xT, Ttok, g2, W_d, ncols, fm, tm, TT=512):
    KC = 16
    R = P.res
    TT = min(TT, Ttok)
    with ExitStack() as es:
        sb = lambda n, s, d: es.enter_context(nc.sbuf_tensor(UQ(n), s, d))
        ps = lambda n, s, d: es.enter_context(nc.psum_tensor(UQ(n), s, d))
        g_sb = sb("g_sb", [128, 16], F32); r_g = R("g")
        ones = sb("ones", [128, 128], BF16); r_ones = R("ones")
        wb = sb("wb", [128, KC, ncols], BF16); r_wd = R("wd"); r_wp = R("wp")
        wst = [sb(f"wst{i}", [128, KC, 128], F32) for i in range(2)]; r_wst = [R(f"wst{i}") for i in range(2)]
        xin = sb("xin", [128, KC, TT], F32); r_xin = R("xin")
        hT = sb("hT", [128, KC, TT], BF16); r_hT = R("hT")
        sq = [sb(f"sq{i}", [128, TT], BF16) for i in range(2)]; r_sq = [R(f"sq{i}") for i in range(2)]
        rstd = sb("rstd", [128, TT], F32); r_rstd = R("rstd")
        ssq = ps("ssq", [128, TT], F32); r_ssq = R("ssq")
        pf = [ps(f"pf{i}", [128, 512], F32) for i in range(3)]; r_pf = [R(f"pf{i}") for i in range(3)]
        stf = [sb(f"stf{i}", [128, 512], F32) for i in range(3)]; r_stf = [R(f"stf{i}") for i in range(3)]
        stb = [sb(f"stb{i}", [128, 512], BF16) for i in range(3)]; r_stb = [R(f"stb{i}") for i in range(3)]
        P.ld(g_sb[:], g2, writes=[r_g])
        P.op("pool", lambda e: e.memset(ones[:], 1.0), writes=[r_ones])
        for idx, c0 in enumerate(range(0, ncols, 128)):
            c1 = min(c0 + 128, ncols)
            b = idx % 2
            n = c1 - c0
            P.ld(wst[b][:, :, 0:n], W_d[:, :, c0:c1], writes=[r_wst[b]], eng="sync" if b == 0 else "act")
            if b == 0:
                P.cp(wb[:, :, c0:c1], wst[b][:, :, 0:n], reads=[r_wst[b]], writes=[r_wd], eng="dve")
            else:
                P.cp(wb[:, :, c0:c1], wst[b][:, :, 0:n], reads=[r_wst[b]], writes=[r_wp], eng="pool")
        rW = [r_wd, r_wp]
        xTv = xT.rearrange("(c p) t -> p c t", p=128)
        k = 0
        for ti in range(Ttok // TT):
            tsl = slice(ti * TT, (ti + 1) * TT)
            P.ld(xin[:], xTv[:, :, tsl], writes=[r_xin])
            for c in range(KC):
                s = c % 2
                P.act(sq[s][:], xin[:, c, :], AF.Square, reads=[r_xin], writes=[r_sq[s]])
                P.mm(ssq[:], ones[:], sq[s][:], c == 0, c == KC - 1, reads=[r_ones, r_sq[s]], writes=[r_ssq])
            P.act(rstd[:], ssq[:], AF.Sqrt, bias=EPS, scale=1.0 / 2048, reads=[r_ssq], writes=[r_rstd])
            P.op("dve", lambda e: e.reciprocal(out=rstd[:], in_=rstd[:]), reads=[r_rstd], writes=[r_rstd])
            for c in range(KC):
                P.stt(hT[:, c, :], xin[:, c, :], g_sb[:, c:c + 1], rstd[:], ALU.mult, ALU.mult,
                      reads=[r_xin, r_g, r_rstd], writes=[r_hT])
            for (c0, m, dst, dty, func, scale) in fm:
                a = k % 3
                k += 1
                for c in range(KC):
                    P.mm(pf[a][0:m, 0:TT], wb[:, c, c0:c0 + m], hT[:, c, :], c == 0, c == KC - 1,
                         reads=rW + [r_hT], writes=[r_pf[a]])
                st_, r_st = (stb[a], r_stb[a]) if dty == BF16 else (stf[a], r_stf[a])
                P.act(st_[0:m, 0:TT], pf[a][0:m, 0:TT], func, scale=scale, reads=[r_pf[a]], writes=[r_st])
                P.st(dst[:, tsl], st_[0:m, 0:TT], reads=[r_st])
            for tb in range(TT // 128):
                tok = slice(tb * 128, (tb + 1) * 128)
                gtok = slice(ti * TT + tb * 128, ti * TT + (tb + 1) * 128)
                for (c0, n, dst, dty, func, scale) in tm:
                    a = k % 3
                    k += 1
                    for c in range(KC):
                        P.mm(pf[a][:, 0:n], hT[:, c, tok], wb[:, c, c0:c0 + n], c == 0, c == KC - 1,
                             reads=rW + [r_hT], writes=[r_pf[a]])
                    st_, r_st = (stb[a], r_stb[a]) if dty == BF16 else (stf[a], r_stf[a])
                    P.act(st_[:, 0:n], pf[a][:, 0:n], func, scale=scale, reads=[r_pf[a]], writes=[r_st])
                    P.st(dst[gtok, :], st_[:, 0:n], reads=[r_st])
        P.barrier()
        P.emit(last=False)


NFA = 2640
NTA = 1024
NFB = 3072
NTC = 2064


def build_cd(T, do_gdn=True, do_dsa=True):
    nc = bass.Bass("TRN2", target_bir_lowering=False)
    NCH = T // 64
    NB = T // 128
    TO = T // 4
    NQ = TO // 128
    din = lambda n, s, d=F32: nc.dram_tensor(n, s, d, kind="ExternalInput").ap()
    xT = din("xT", [2048, T]); xTo = din("xTo", [2048, TO]); g2 = din("g2", [128, 16])
    WA = din("WA", [128, 16, NFA + NTA]); WB = din("WB", [128, 16, NFB]); WC = din("WC", [128, 16, NTC])
    gconvw_d = din("gconvw", [128, 16, 4])
    gdp_d = din("gdp", [8, 2])
    ggain_d = din("ggain", [64, 1024])
    ident_d = din("ident", [128, 128])
    m64_d = din("m64", [64, 3, 512])
    eye8_d = din("eye8", [64, 512])
    kvg_d = din("kvg", [128, 4])
    wuk_d = din("wuk", [128, 4, 2048]); wuv_d = din("wuv", [128, 4, 2048])
    relb_d = din("relb", [32, 16]); ohm_d = din("ohm", [32, 768])
    adm_d = din("admneg", [128, 512]); J_d = din("J", [128, 128])
    gidx_d = din("gidx", [1, 1])
    yd = nc.dram_tensor("yd", [T, 1024], BF16, kind="ExternalOutput").ap()
    yc = nc.dram_tensor("yc", [TO, 2048], BF16, kind="ExternalOutput").ap()
    sc = lambda n, s, d: nc.dram_tensor(n, s, d).ap()
    gqkv_s = sc("gqkv_s", [2048, T], F32); ckvT_s = sc("ckvT_s", [512, T], F32); ikT_s = sc("ikT_s", [64, T], BF16)
    ba_s = sc("ba_s", [16, T], F32); zd_s = sc("zd_s", [T, 1024], F32)
    qcT_s = sc("qcT_s", [2048, TO], BF16); iqT_s = sc("iqT_s", [1024, TO], BF16)
    zc_s = sc("zc_s", [TO, 2048], F32); iw_s = sc("iw_s", [TO, 16], F32)
    gc_h = nc.dram_tensor("gc_s", [8, T], F32); LB_h = nc.dram_tensor("LB_s", [8, T], F32)
    qnT_s = sc("qnT_s", [512, T], F32); knT_s = sc("knT_s", [512, T], F32)
    ktm_s = sc("ktm_s", [T, 512], F32); vtm_s = sc("vtm_s", [T, 1024], F32)
    KT_s = sc("KT_s", [16, 128, T], BF16); V_s = sc("V_s", [T, 2048], BF16)
    tab_h = nc.dram_tensor("tab_s", [16, 768], F32)

    with ExitStack() as es0:
        P = Prog(nc, es0)
        R = P.res
        fmA = []
        for j in range(16):
            fmA.append((j * 128, 128, gqkv_s[j * 128:(j + 1) * 128], F32, AF.Copy, 1.0))
        for j in range(4):
            fmA.append((2048 + j * 128, 128, ckvT_s[j * 128:(j + 1) * 128], F32, AF.Copy, 1.0))
        fmA.append((2560, 64, ikT_s[0:64], BF16, AF.Copy, 1.0))
        fmA.append((2624, 8, ba_s[0:8], F32, AF.Copy, 1.0))
        fmA.append((2632, 8, ba_s[8:16], F32, AF.Copy, 1.0))
        tmA = [(NFA, 512, zd_s[:, 0:512], F32, AF.Silu, 1.0), (NFA + 512, 512, zd_s[:, 512:1024], F32, AF.Silu, 1.0)]
        inproj_pass(nc, P, xT, T, g2, WA, NFA + NTA, fmA if (do_gdn or do_dsa) else [], tmA if do_gdn else [])
        if do_dsa:
            fmB = [(j * 128, 128, qcT_s[j * 128:(j + 1) * 128], BF16, AF.Copy, 1.0) for j in range(16)]
            fmB += [(2048 + j * 128, 128, iqT_s[j * 128:(j + 1) * 128], BF16, AF.Copy, 1.0) for j in range(8)]
            inproj_pass(nc, P, xTo, TO, g2, WB, NFB, fmB, [])
            tmC = [(j * 512, 512, zc_s[:, j * 512:(j + 1) * 512], F32, AF.Silu, 1.0) for j in range(4)]
            tmC.append((2048, 16, iw_s[:, 0:16], F32, AF.Copy, 1.0 / 32))
            inproj_pass(nc, P, xTo, TO, g2, WC, NTC, [], tmC)
        if do_gdn:
            gdn_phases(nc, P, T, ba_s, gdp_d, gc_h, LB_h, gqkv_s, gconvw_d, qnT_s, knT_s, ktm_s, vtm_s, zd_s,
                       ggain_d, ident_d, m64_d, eye8_d, yd, last=not do_dsa)
        if do_dsa:
            dsa_phases(nc, P, T, ckvT_s, kvg_d, wuk_d, wuv_d, KT_s, V_s, ikT_s, iqT_s, iw_s, qcT_s, zc_s,
                       relb_d, ohm_d, tab_h, ident_d, adm_d, J_d, yc)
        print("nsem", P.nsem, "counts", P.ccnt)
    return nc


def gdn_phases(nc, P, T, ba_s, gdp_d, gc_h, LB_h, gqkv_s, gconvw_d, qnT_s, knT_s, ktm_s, vtm_s, zd_s,
               ggain_d, ident_d, m64_d, eye8_d, yd, last):
    R = P.res
    NCH = T // 64
    TT = min(512, T)
    NTI = T // TT
    gc_s = gc_h.ap(); LB_s = LB_h.ap()
    with ExitStack() as es:
        sb = lambda n, s, d: es.enter_context(nc.sbuf_tensor(UQ(n), s, d))
        ps = lambda n, s, d: es.enter_context(nc.psum_tensor(UQ(n), s, d))
        gc_tm = sb("gc_tm", [64, NCH, 8], F32); r_gctm = R("gctm")
        LB_tm = sb("LB_tm", [64, NCH, 8], F32); r_LBtm = R("LBtm")
        be_tm = sb("be_tm", [64, NCH, 8], F32); r_betm = R("betm")
        egc_tm = sb("egc_tm", [64, NCH, 8], F32); r_egctm = R("egctm")
        bege_tm = sb("bege_tm", [64, NCH, 8], F32); r_begetm = R("begetm")
        ident = sb("identg", [128, 128], F32); r_id = R("idg", True)
        P.ld(ident[:], ident_d, writes=[r_id])
        with ExitStack() as es2:
            sb2 = lambda n, s, d: es2.enter_context(nc.sbuf_tensor(UQ(n), s, d))
            ps2 = lambda n, s, d: es2.enter_context(nc.psum_tensor(UQ(n), s, d))
            rb = sb2("rb", [8, T], F32); r_rb = R("rb")
            ra = sb2("ra", [8, T], F32); r_ra = R("ra")
            rbe = sb2("rbe", [8, T], F32); r_rbe = R("rbe")
            rgc = sb2("rgc", [8, T], F32); r_rgc = R("rgc")
            gdp = sb2("gdp", [8, 2], F32); r_gdp = R("gdp")
            aneg = sb2("aneg", [8, 1], F32); r_aneg = R("aneg")
            P.ld(rb[:], ba_s[0:8, :], writes=[r_rb]); P.ld(ra[:], ba_s[8:16, :], writes=[r_ra], eng="act")
            P.ld(gdp[:], gdp_d, writes=[r_gdp])
            P.act(aneg[:], gdp[:, 1:2], AF.Exp, reads=[r_gdp], writes=[r_aneg])
            P.ts(aneg[:], aneg[:], -1.0, ALU.mult, reads=[r_aneg], writes=[r_aneg])
            P.act(rbe[:], rb[:], AF.Sigmoid, reads=[r_rb], writes=[r_rbe])
            P.act(rb[:], rb[:], AF.Exp, scale=-1.0, reads=[r_rb], writes=[r_rb])
            P.act(rb[:], rb[:], AF.Ln, bias=1.0, reads=[r_rb], writes=[r_rb])
            P.act(ra[:], ra[:], AF.Exp, bias=gdp[:, 0:1], reads=[r_ra, r_gdp], writes=[r_ra])
            P.act(ra[:], ra[:], AF.Ln, bias=1.0, reads=[r_ra], writes=[r_ra])
            P.ts(ra[:], ra[:], aneg[:, 0:1], ALU.mult, reads=[r_ra, r_aneg], writes=[r_ra])
            for c in range(NCH):
                cs = slice(c * 64, (c + 1) * 64)
                P.op("dve", lambda e, cs=cs: e.tensor_tensor_scan(out=rgc[:, cs], data0=ra[:, cs], data1=ra[:, cs], initial=0.0,
                                                                  op0=ALU.add, op1=ALU.min), reads=[r_ra], writes=[r_rgc])
            P.tt(rb[:], rgc[:], rb[:], ALU.subtract, reads=[r_rgc, r_rb], writes=[r_rb])
            P.st(gc_s, rgc[:], reads=[r_rgc]); P.st(LB_s, rb[:], reads=[r_rb])
            for (src, r_src, dst, r_dst) in ((rgc, r_rgc, gc_tm, r_gctm), (rb, r_rb, LB_tm, r_LBtm), (rbe, r_rbe, be_tm, r_betm)):
                for c0 in range(0, NCH, 64):
                    c1 = min(NCH, c0 + 64)
                    pt = ps2("ptg", [64, 512], F32); r_pt = R("ptg")
                    for c in range(c0, c1):
                        P.tr(pt[:, (c - c0) * 8:(c - c0 + 1) * 8], src[0:8, c * 64:(c + 1) * 64], ident[0:8, 0:8],
                             reads=[r_src, r_id], writes=[r_pt])
                    P.cp(dst[:, c0:c1, :].rearrange("p c h -> p (c h)"), pt[:, 0:(c1 - c0) * 8], reads=[r_pt], writes=[r_dst])
            P.act(egc_tm[:], gc_tm[:], AF.Exp, reads=[r_gctm], writes=[r_egctm])
            P.tt(bege_tm[:], egc_tm[:], be_tm[:], ALU.mult, reads=[r_egctm, r_betm], writes=[r_begetm])
            P.barrier()
            P.emit(last=False)
        with ExitStack() as es2:
            sb2 = lambda n, s, d: es2.enter_context(nc.sbuf_tensor(UQ(n), s, d))
            ps2 = lambda n, s, d: es2.enter_context(nc.psum_tensor(UQ(n), s, d))
            convw = sb2("gconvw", [128, 16, 4], F32); r_cw = R("gcw")
            onesb = sb2("onesb", [128, 128], BF16); r_ones = R("gones")
            pre = [sb2(f"gpre{i}", [128, 3 + TT], F32) for i in range(2)]; r_pre = [R(f"gpre{i}") for i in range(2)]
            cacc = [sb2(f"gcacc{i}", [128, TT], F32) for i in range(2)]; r_cacc = [R(f"gcacc{i}") for i in range(2)]
            cs_ = [sb2(f"gcs{i}", [128, TT], F32) for i in range(2)]; r_cs = [R(f"gcs{i}") for i in range(2)]
            sqb = sb2("gsq", [128, TT], BF16); r_sq = R("gsq")
            pss = ps2("gpss", [128, TT], F32); r_pss = R("gpss")
            rst = sb2("grst", [128, TT], F32); r_rst = R("grst")
            ptr = [ps2(f"gptr{i}", [64, 4, 128], F32) for i in range(2)]; r_ptr = [R(f"gptr{i}") for i in range(2)]
            stg = [sb2(f"gstg{i}", [64, 4, 128], F32) for i in range(2)]; r_stg = [R(f"gstg{i}") for i in range(2)]
            P.ld(convw[:], gconvw_d, writes=[r_cw])
            P.op("pool", lambda e: e.memset(onesb[:], 1.0), writes=[r_ones])
            ktv = ktm_s.rearrange("(c l) (h d) -> l c h d", l=64, d=128)
            vtv = vtm_s.rearrange("(c l) (h d) -> l c h d", l=64, d=128)
            k = 0
            kk = 0
            for ti in range(NTI):
                tsl = slice(ti * TT, (ti + 1) * TT)
                for blk in range(16):
                    a = k % 2
                    k += 1
                    rows = slice(blk * 128, (blk + 1) * 128)
                    if ti == 0:
                        P.op("pool", lambda e, a=a: e.memset(pre[a][:, 0:3], 0.0), writes=[r_pre[a]])
                        P.ld(pre[a][:, 3:3 + TT], gqkv_s[rows, 0:TT], writes=[r_pre[a]], eng="sync")
                    else:
                        P.ld(pre[a][:], gqkv_s[rows, ti * TT - 3:(ti + 1) * TT], writes=[r_pre[a]], eng="sync")
                    P.ts(cacc[a][:], pre[a][:, 0:TT], convw[:, blk, 0:1], ALU.mult, reads=[r_pre[a], r_cw], writes=[r_cacc[a]])
                    for tap in range(1, 4):
                        P.stt(cacc[a][:], pre[a][:, tap:tap + TT], convw[:, blk, tap:tap + 1], cacc[a][:], ALU.mult, ALU.add,
                              reads=[r_pre[a], r_cw, r_cacc[a]], writes=[r_cacc[a]])
                    P.act(cs_[a][:], cacc[a][:], AF.Silu, reads=[r_cacc[a]], writes=[r_cs[a]])
                    if blk < 8:
                        P.act(sqb[:], cs_[a][:], AF.Square, reads=[r_cs[a]], writes=[r_sq])
                        P.mm(pss[:], onesb[:], sqb[:], True, True, reads=[r_ones, r_sq], writes=[r_pss])
                        P.act(rst[:], pss[:], AF.Sqrt, bias=EPS, scale=1.0, reads=[r_pss], writes=[r_rst])
                        P.op("dve", lambda e: e.reciprocal(out=rst[:], in_=rst[:]), reads=[r_rst], writes=[r_rst])
                        if blk < 4:
                            P.stt(cs_[a][:], cs_[a][:], 128.0 ** -0.5, rst[:], ALU.mult, ALU.mult, reads=[r_cs[a], r_rst], writes=[r_cs[a]])
                            P.st(qnT_s[rows, tsl], cs_[a][:], reads=[r_cs[a]], eng="act")
                        else:
                            P.tt(cs_[a][:], cs_[a][:], rst[:], ALU.mult, reads=[r_cs[a], r_rst], writes=[r_cs[a]])
                            P.st(knT_s[blk * 128 - 512:blk * 128 - 384, tsl], cs_[a][:], reads=[r_cs[a]], eng="act")
                    if blk >= 4:
                        dstv, hh = (ktv, blk - 4) if blk < 8 else (vtv, blk - 8)
                        for c4 in range(0, TT // 64, 4):
                            pa = kk % 2
                            kk += 1
                            for q in range(4):
                                cc = c4 + q
                                P.tr(ptr[pa][:, q, :], cs_[a][:, cc * 64:(cc + 1) * 64], ident[:], reads=[r_cs[a], r_id], writes=[r_ptr[pa]])
                            P.cp(stg[pa][:], ptr[pa][:], reads=[r_ptr[pa]], writes=[r_stg[pa]], eng="act")
                            cg = ti * (TT // 64) + c4
                            P.st(dstv[:, cg:cg + 4, hh, :], stg[pa][:], reads=[r_stg[pa]], eng="pool")
            P.barrier()
            P.emit(last=False)
        with ExitStack() as es2:
            sb2 = lambda n, s, d: es2.enter_context(nc.sbuf_tensor(UQ(n), s, d))
            ps2 = lambda n, s, d: es2.enter_context(nc.psum_tensor(UQ(n), s, d))
            m64 = sb2("m64", [64, 3, 512], F32); r_m64 = R("m64")
            eye8 = sb2("eye8", [64, 512], F32); r_eye = R("eye8")
            ggain = sb2("ggain", [64, 1024], F32); r_gg = R("ggain")
            S = sb2("S", [128, 8, 128], F32); r_S = [R(f"S{i}") for i in range(2)]
            P.ld(m64[:], m64_d, writes=[r_m64]); P.ld(eye8[:], eye8_d, writes=[r_eye]); P.ld(ggain[:], ggain_d, writes=[r_gg])
            P.op("pool", lambda e: e.memset(S[:], 0.0), writes=r_S)
            NP = 2
            def mk(nm, shape, dt=F32):
                return [sb2(f"{nm}{i}", shape, dt) for i in range(NP)], [R(f"{nm}{i}") for i in range(NP)]
            gcb, r_gcb = mk("gcb", [128, 8, 64]); LBb, r_LBb = mk("LBb", [64, 8, 64])
            qT, r_qT = mk("gqT", [128, 4, 64]); kT, r_kT = mk("gkT", [128, 4, 64])
            ktm, r_ktm = mk("gktm", [64, 4, 128]); vtm, r_vtm = mk("gvtm", [64, 8, 128]); zdt, r_zdt = mk("gzd", [64, 1024])
            e1, r_e1 = mk("ge1", [64, 512]); Xs, r_X = mk("gX", [64, 512]); Ys, r_Y = mk("gY", [64, 512])
            X2, r_X2 = mk("gX2", [64, 512]); Y2, r_Y2 = mk("gY2", [64, 512])
            Ps, r_Ps = mk("gP", [64, 512]); Qs, r_Qs = mk("gQ", [64, 512])
            QK, r_QK = mk("gQK", [64, 512])
            vb, r_vb = mk("gvb", [64, 8, 128]); kb, r_kb = mk("gkb", [64, 8, 128]); kd, r_kd = mk("gkd", [64, 8, 128])
            ekd, r_ekd = mk("gekd", [64, 8]); egl, r_egl = mk("gegl", [128, 8])
            WT, r_WT = mk("gWT", [128, 8, 64]); U0, r_U0 = mk("gU0", [64, 8, 128])
            u_, r_u = mk("gu", [64, 8, 128]); o_, r_o = mk("go", [64, 8, 128])
            osq = sb2("gosq", [64, 8, 128], F32); r_osq = R("gosq")
            ssm = sb2("gssm", [64, 8], F32); r_ssm = R("gssm")
            yo, r_yo = mk("gyo", [64, 1024], BF16)
            pkq = ps2("pkq", [64, 8, 64], F32); r_pkq = R("pkq")
            pd = [ps2(f"pd{i}", [64, 512], F32) for i in range(2)]; r_pd = [R(f"pd{i}") for i in range(2)]
            pwt = ps2("pwt", [128, 8, 64], F32); r_pwt = R("pwt")
            pA = ps2("pA", [64, 4, 128], F32); r_pA = R("pA")
            pB = ps2("pB", [64, 4, 128], F32); r_pB = R("pB")
            pC = ps2("pC", [64, 4, 128], F32); r_pC = R("pC")
            pD = ps2("pD", [128, 4, 128], F32); r_pD = R("pD")
            h8 = lambda ap: ap.rearrange("p (h l) -> p h l", h=8)
            strict8 = m64[:, 0, :]; strictT8 = m64[:, 1, :]; incl8 = m64[:, 2, :]
            qv = qnT_s.rearrange("(h p) t -> p h t", p=128); kv = knT_s.rearrange("(h p) t -> p h t", p=128)
            pdk = [0]

            def chunk_gen(c):
                b = c % NP
                cs = slice(c * 64, (c + 1) * 64)
                P.ld(gcb[b][:], bass.AP(gc_h, c * 64, [[0, 128], [T, 8], [1, 64]]), writes=[r_gcb[b]], eng="pool")
                P.ld(LBb[b][:], bass.AP(LB_h, c * 64, [[0, 64], [T, 8], [1, 64]]), writes=[r_LBb[b]], eng="pool")
                P.ld(qT[b][:], qv[:, :, cs], writes=[r_qT[b]], eng="sync")
                P.ld(kT[b][:], kv[:, :, cs], writes=[r_kT[b]], eng="sync")
                P.ld(ktm[b][:], ktm_s[cs, :].rearrange("l (h d) -> l h d", d=128), writes=[r_ktm[b]], eng="sync")
                P.ld(vtm[b][:], vtm_s[cs, :].rearrange("l (h d) -> l h d", d=128), writes=[r_vtm[b]], eng="act")
                P.ld(zdt[b][:], zd_s[cs, :], writes=[r_zdt[b]], eng="act")
                for hq in range(4):
                    P.mm(pkq[:, hq, :], kT[b][:, hq, :], kT[b][:, hq, :], True, True, reads=[r_kT[b]], writes=[r_pkq])
                for hq in range(4):
                    P.mm(pkq[:, 4 + hq, :], kT[b][:, hq, :], qT[b][:, hq, :], True, True, reads=[r_kT[b], r_qT[b]], writes=[r_pkq])
                gct_b = gc_tm[:, c, :].unsqueeze(2).to_broadcast([64, 8, 64])
                LBt_b = LB_tm[:, c, :].unsqueeze(2).to_broadcast([64, 8, 64])
                kk_b = pkq[:, 0:4, :].unsqueeze(2).to_broadcast([64, 4, 2, 64])
                qk_b = pkq[:, 4:8, :].unsqueeze(2).to_broadcast([64, 4, 2, 64])
                v42 = lambda ap: ap.rearrange("p (a b l) -> p a b l", a=4, b=2)
                P.tt(e1[b][:], LBb[b][:].rearrange("p h l -> p (h l)"), strict8, ALU.add, reads=[r_LBb[b], r_m64], writes=[r_e1[b]])
                P.tt(h8(e1[b][:]), h8(e1[b][:]), gct_b, ALU.subtract, reads=[r_e1[b], r_gctm], writes=[r_e1[b]])
                P.act(e1[b][:], e1[b][:], AF.Exp, reads=[r_e1[b]], writes=[r_e1[b]])
                P.tt(v42(Xs[b][:]), v42(e1[b][:]), kk_b, ALU.mult, reads=[r_e1[b], r_pkq], writes=[r_X[b]])
                P.tt(e1[b][:], strictT8, gcb[b][0:64].rearrange("p h l -> p (h l)"), ALU.subtract, reads=[r_gcb[b], r_m64, r_X[b]], writes=[r_e1[b]])
                P.tt(h8(e1[b][:]), h8(e1[b][:]), LBt_b, ALU.add, reads=[r_e1[b], r_LBtm], writes=[r_e1[b]])
                P.act(e1[b][:], e1[b][:], AF.Exp, reads=[r_e1[b]], writes=[r_e1[b]])
                P.tt(v42(Ys[b][:]), v42(e1[b][:]), kk_b, ALU.mult, reads=[r_e1[b], r_pkq], writes=[r_Y[b]])
                P.tt(e1[b][:], gcb[b][0:64].rearrange("p h l -> p (h l)"), incl8, ALU.add, reads=[r_gcb[b], r_m64, r_Y[b]], writes=[r_e1[b]])
                P.tt(h8(e1[b][:]), h8(e1[b][:]), gct_b, ALU.subtract, reads=[r_e1[b], r_gctm], writes=[r_e1[b]])
                P.act(e1[b][:], e1[b][:], AF.Exp, reads=[r_e1[b]], writes=[r_e1[b]])
                P.tt(v42(QK[b][:]), v42(e1[b][:]), qk_b, ALU.mult, reads=[r_e1[b], r_pkq], writes=[r_QK[b]])
                P.tt(vb[b][:], vtm[b][:], be_tm[:, c, :].unsqueeze(2).to_broadcast([64, 8, 128]), ALU.mult,
                     reads=[r_vtm[b], r_betm], writes=[r_vb[b]], eng="pool")
                k42 = ktm[b][:].unsqueeze(2).to_broadcast([64, 4, 2, 128])
                d42 = lambda ap: ap.rearrange("p (a b) d -> p a b d", a=4, b=2)
                P.tt(d42(kb[b][:]), k42, bege_tm[:, c, :].rearrange("p (a b) -> p a b", a=4).unsqueeze(3).to_broadcast([64, 4, 2, 128]),
                     ALU.mult, reads=[r_ktm[b], r_begetm], writes=[r_kb[b]], eng="pool")
                P.tt(ekd[b][:], gcb[b][0:64, :, 63], gc_tm[:, c, :], ALU.subtract, reads=[r_gcb[b], r_gctm], writes=[r_ekd[b]])
                P.act(ekd[b][:], ekd[b][:], AF.Exp, reads=[r_ekd[b]], writes=[r_ekd[b]])
                P.act(egl[b][:], gcb[b][:, :, 63], AF.Exp, reads=[r_gcb[b]], writes=[r_egl[b]])
                P.tt(d42(kd[b][:]), k42, ekd[b][:].rearrange("p (a b) -> p a b", a=4).unsqueeze(3).to_broadcast([64, 4, 2, 128]),
                     ALU.mult, reads=[r_ktm[b], r_ekd[b]], writes=[r_kd[b]], eng="pool")
                yield
                P.tt(Ps[b][:], eye8[:], Xs[b][:], ALU.subtract, reads=[r_eye, r_X[b]], writes=[r_Ps[b]])
                P.tt(Qs[b][:], eye8[:], Ys[b][:], ALU.subtract, reads=[r_eye, r_Y[b]], writes=[r_Qs[b]])
                Xc, rXc, Yc, rYc = Xs[b], r_X[b], Ys[b], r_Y[b]
                Xn, rXn, Yn, rYn = X2[b], r_X2[b], Y2[b], r_Y2[b]
                for it in range(5):
                    lastit = (it == 4)
                    a = pdk[0] % 2; pdk[0] += 1
                    for h in range(8):
                        hs = slice(h * 64, (h + 1) * 64)
                        P.mm(pd[a][:, hs], Yc[:, hs], Xc[:, hs], True, True, reads=[rYc, rXc], writes=[r_pd[a]])
                    P.cp(Xn[:], pd[a][:], reads=[r_pd[a]], writes=[rXn])
                    if not lastit:
                        a = pdk[0] % 2; pdk[0] += 1
                        for h in range(8):
                            hs = slice(h * 64, (h + 1) * 64)
                            P.mm(pd[a][:, hs], Xc[:, hs], Yc[:, hs], True, True, reads=[rYc, rXc], writes=[r_pd[a]])
                        P.cp(Yn[:], pd[a][:], reads=[r_pd[a]], writes=[rYn], eng="act")
                    yield
                    a = pdk[0] % 2; pdk[0] += 1
                    for h in range(8):
                        hs = slice(h * 64, (h + 1) * 64)
                        P.mm(pd[a][:, hs], Qs[b][:, hs], Xn[:, hs], True, True, reads=[r_Qs[b], rXn], writes=[r_pd[a]])
                    if not lastit:
                        a2 = pdk[0] % 2; pdk[0] += 1
                        for h in range(8):
                            hs = slice(h * 64, (h + 1) * 64)
                            P.mm(pd[a2][:, hs], Ps[b][:, hs], Yn[:, hs], True, True, reads=[r_Ps[b], rYn], writes=[r_pd[a2]])
                    P.tt(Ps[b][:], Ps[b][:], pd[a][:], ALU.add, reads=[r_Ps[b], r_pd[a]], writes=[r_Ps[b]])
                    if not lastit:
                        P.tt(Qs[b][:], Qs[b][:], pd[a2][:], ALU.add, reads=[r_Qs[b], r_pd[a2]], writes=[r_Qs[b]])
                    Xc, rXc, Yc, rYc, Xn, rXn, Yn, rYn = Xn, rXn, Yn, rYn, Xc, rXc, Yc, rYc
                    yield
                for h in range(8):
                    hs = slice(h * 64, (h + 1) * 64)
                    P.mm(pwt[:, h, :], kb[b][:, h, :], Ps[b][:, hs], True, True, reads=[r_kb[b], r_Ps[b]], writes=[r_pwt])
                P.cp(WT[b][:], pwt[:], reads=[r_pwt], writes=[r_WT[b]], eng="act")
                for half in range(2):
                    for h4 in range(4):
                        h = half * 4 + h4
                        hs = slice(h * 64, (h + 1) * 64)
                        P.mm(pA[:, h4, :], Ps[b][:, hs], vb[b][:, h, :], True, True, reads=[r_Ps[b], r_vb[b]], writes=[r_pA])
                    P.cp(U0[b][:, half * 4:half * 4 + 4, :], pA[:], reads=[r_pA], writes=[r_U0[b]], eng="act")
                yield

            def chunk_seq(c):
                b = c % NP
                cs = slice(c * 64, (c + 1) * 64)
                for half in range(2):
                    hsl = slice(half * 4, half * 4 + 4)
                    for h4 in range(4):
                        h = half * 4 + h4
                        P.mm(pA[:, h4, :], WT[b][:, h, :], S[:, h, :], True, True, reads=[r_WT[b], r_S[half]], writes=[r_pA])
                    P.tt(u_[b][:, hsl, :], U0[b][:, hsl, :], pA[:], ALU.subtract, reads=[r_U0[b], r_pA], writes=[r_u[b]])
                    for h4 in range(4):
                        h = half * 4 + h4
                        P.mm(pB[:, h4, :], qT[b][:, h // 2, :], S[:, h, :], True, True, reads=[r_qT[b], r_S[half]], writes=[r_pB])
                    for h4 in range(4):
                        h = half * 4 + h4
                        P.mm(pC[:, h4, :], QK[b][:, h * 64:(h + 1) * 64], u_[b][:, h, :], True, True, reads=[r_QK[b], r_u[b]], writes=[r_pC])
                    P.tt(o_[b][:, hsl, :], pB[:], egc_tm[:, c, hsl].unsqueeze(2).to_broadcast([64, 4, 128]), ALU.mult,
                         reads=[r_pB, r_egctm], writes=[r_o[b]])
                    P.tt(o_[b][:, hsl, :], o_[b][:, hsl, :], pC[:], ALU.add, reads=[r_o[b], r_pC], writes=[r_o[b]])
                    for h4 in range(4):
                        h = half * 4 + h4
                        P.mm(pD[:, h4, :], kd[b][:, h, :], u_[b][:, h, :], True, True, reads=[r_kd[b], r_u[b]], writes=[r_pD])
                    P.tt(S[:, hsl, :], S[:, hsl, :], egl[b][:, hsl].unsqueeze(2).to_broadcast([128, 4, 128]), ALU.mult,
                         reads=[r_S[half], r_egl[b]], writes=[r_S[half]])
                    P.tt(S[:, hsl, :], S[:, hsl, :], pD[:], ALU.add, reads=[r_S[half], r_pD], writes=[r_S[half]])
                P.tt(osq[:], o_[b][:], o_[b][:], ALU.mult, reads=[r_o[b]], writes=[r_osq], eng="pool")
                P.op("dve", lambda e: e.tensor_reduce(out=ssm[:], in_=osq[:], axis=AX.X, op=ALU.add), reads=[r_osq], writes=[r_ssm])
                P.act(ssm[:], ssm[:], AF.Sqrt, bias=EPS, scale=1.0 / 128, reads=[r_ssm], writes=[r_ssm])
                P.op("dve", lambda e: e.reciprocal(out=ssm[:], in_=ssm[:]), reads=[r_ssm], writes=[r_ssm])
                P.tt(zdt[b][:], zdt[b][:], ggain[:], ALU.mult, reads=[r_zdt[b], r_gg], writes=[r_zdt[b]], eng="pool")
                P.tt(o_[b][:], o_[b][:], ssm[:].unsqueeze(2).to_broadcast([64, 8, 128]), ALU.mult, reads=[r_o[b], r_ssm], writes=[r_o[b]])
                P.tt(yo[b][:], o_[b][:].rearrange("p h d -> p (h d)"), zdt[b][:], ALU.mult, reads=[r_o[b], r_zdt[b]], writes=[r_yo[b]])
                P.st(yd[cs, :], yo[b][:], reads=[r_yo[b]], final=True)

            gens = {}
            for c in range(NCH + 1):
                if c < NCH:
                    gens[c] = chunk_gen(c)
                if c < NCH:
                    for _ in gens[c]:
                        pass
                if c >= 1:
                    chunk_seq(c - 1)
            P.barrier()
            P.emit(last=last)


def t5_bucket_np(rel):
    nb = 16; max_exact = 8
    ret = np.where(rel > 0, nb, 0)
    n = np.abs(rel)
    nf = np.maximum(n, max_exact).astype(np.float32)
    large = max_exact + (np.log(nf / max_exact) / np.log(128 / max_exact) * (nb - max_exact)).astype(np.int32)
    large = np.minimum(large, nb - 1)
    return ret + np.where(n < max_exact, n, large)


def cd_inputs(inp, x1, b, g, T):
    f32 = np.float32
    W = inp["cd_w_in"][0]
    cols = lambda off, n: W[:, off:off + n]
    lay = lambda A: np.ascontiguousarray(A.reshape(16, 128, A.shape[1]).transpose(1, 0, 2))
    WA = np.concatenate([cols(5712 + 512 * g, 512), cols(7760 + 512 * g, 512), cols(9808 + 1024 * g, 1024),
                         cols(2048, 512), cols(5632, 64), cols(18000 + 8 * g, 8), cols(18032 + 8 * g, 8),
                         cols(13904 + 1024 * g, 1024)], axis=1)
    WB = np.concatenate([cols(0, 2048), cols(4608, 1024)], axis=1)
    WC = np.concatenate([cols(2560, 2048), cols(5696, 16)], axis=1)
    xT = np.ascontiguousarray(x1[b, :T].T)
    NQ = T // 512
    own = np.concatenate([np.arange((4 * k + g) * 128, (4 * k + g) * 128 + 128) for k in range(NQ)])
    cw = inp["cd_gdn_conv_w"][0]
    ch = np.concatenate([np.arange(512 * g, 512 * g + 512), 2048 + np.arange(512 * g, 512 * g + 512),
                         4096 + np.arange(1024 * g, 1024 * g + 1024)])
    gconvw = np.ascontiguousarray(cw[:, ch].T.reshape(16, 128, 4).transpose(1, 0, 2))
    hs = slice(8 * g, 8 * g + 8)
    i64 = np.arange(64)
    strict = np.where(i64[:, None] < i64[None, :], 0.0, -1e30)
    strictT = np.where(i64[None, :] < i64[:, None], 0.0, -1e30)
    incl = np.where(i64[:, None] <= i64[None, :], 0.0, -1e30)
    m64 = np.stack([np.tile(strict, (1, 8)), np.tile(strictT, (1, 8)), np.tile(incl, (1, 8))], axis=1)
    rel = np.arange(768) - 255 - 128 * g
    bk_ = t5_bucket_np(rel)
    ohm = np.zeros((32, 768), f32)
    ohm[bk_, np.arange(768)] = 1.0
    ohm[15, :] -= 1.0
    qq = np.arange(128)
    lim = 128 * g + (qq // 64 + 1) * 64
    admneg = np.where(np.arange(512)[None, :] < lim[:, None], 0.0, -3e38).astype(f32)
    d = {
        "xT": xT, "xTo": np.ascontiguousarray(xT[:, own]),
        "g2": np.ascontiguousarray(inp["cd_norm"][0].reshape(16, 128).T),
        "WA": lay(WA), "WB": lay(WB), "WC": lay(WC),
        "gconvw": gconvw,
        "gdp": np.ascontiguousarray(np.stack([inp["cd_gdn_dt_bias"][0][hs], inp["cd_gdn_a_log"][0][hs]], axis=1)),
        "ggain": np.ascontiguousarray(np.broadcast_to(np.tile(inp["cd_gdn_norm"][0], 8), (64, 1024))),
        "ident": np.eye(128, dtype=f32),
        "m64": m64, "eye8": np.tile(np.eye(64), (1, 8)),
        "kvg": np.ascontiguousarray(inp["cd_kv_norm"][0].reshape(4, 128).T),
        "wuk": np.ascontiguousarray(inp["cd_w_uk"][0].reshape(4, 128, 2048).transpose(1, 0, 2)),
        "wuv": np.ascontiguousarray(inp["cd_w_uv"][0].reshape(4, 128, 2048).transpose(1, 0, 2)),
        "relb": inp["rel_bias"], "ohm": ohm, "gidx": np.zeros((1, 1), f32),
        "admneg": admneg, "J": np.eye(128, dtype=f32)[::-1],
    }
    return {k_: np.ascontiguousarray(np.asarray(v_, f32)) for k_, v_ in d.items()}


def dsa_phases(nc, P, T, ckvT_s, kvg_d, wuk_d, wuv_d, KT_s, V_s, ikT_s, iqT_s, iw_s, qcT_s, zc_s,
               relb_d, ohm_d, tab_h, ident_d, adm_d, J_d, yc):
    R = P.res
    TT = min(512, T)
    NTI = T // TT
    TO = T // 4
    NQ = TO // 128
    NKT = T // 128
    tab_s = tab_h.ap()
    DBG = False
    if DBG:
        dbg_rb = nc.dram_tensor("dbg_rb", [128, 16, 640], BF16, kind="ExternalOutput").ap()
        dbg_mb = nc.dram_tensor("dbg_mb", [NQ, 128, T], BF16, kind="ExternalOutput").ap()
        dbg_idx = nc.dram_tensor("dbg_idx", [NQ, 128, T], F32, kind="ExternalOutput").ap()
        dbg_thr = nc.dram_tensor("dbg_thr", [NQ, 128, 8], F32, kind="ExternalOutput").ap()
    with ExitStack() as es:
        sb = lambda n, s, d: es.enter_context(nc.sbuf_tensor(UQ(n), s, d))
        ps = lambda n, s, d: es.enter_context(nc.psum_tensor(UQ(n), s, d))
        ones = sb("kones", [128, 128], BF16); r_ones = R("kones")
        kvg = sb("kvg", [128, 4], F32); r_kvg = R("kvg")
        wst = [sb(f"kwst{i}", [128, 4, 512], F32) for i in range(2)]; r_wst = [R(f"kwst{i}") for i in range(2)]
        wukb = sb("wukb", [128, 4, 2048], BF16); wuvb = sb("wuvb", [128, 4, 2048], BF16)
        r_wd = R("kwd"); r_wp = R("kwp")
        ckT = sb("ckT", [128, 4, TT], F32); r_ckT = R("ckT")
        sq = [sb(f"ksq{i}", [128, TT], BF16) for i in range(2)]; r_sq = [R(f"ksq{i}") for i in range(2)]
        ssq = ps("kssq", [128, TT], F32); r_ssq = R("kssq")
        rstd = sb("krstd", [128, TT], F32); r_rstd = R("krstd")
        cn = sb("cn", [128, 4, TT], BF16); r_cn = R("cn")
        pk = [ps(f"pk{i}", [128, 512], F32) for i in range(3)]; r_pk = [R(f"pk{i}") for i in range(3)]
        stb = [sb(f"kstb{i}", [128, 512], BF16) for i in range(3)]; r_stb = [R(f"kstb{i}") for i in range(3)]
        P.op("pool", lambda e: e.memset(ones[:], 1.0), writes=[r_ones])
        P.ld(kvg[:], kvg_d, writes=[r_kvg])
        wi = 0
        for (src, dst) in ((wuk_d, wukb), (wuv_d, wuvb)):
            for nb in range(4):
                s = wi % 2
                wi += 1
                P.ld(wst[s][:], src[:, :, nb * 512:(nb + 1) * 512], writes=[r_wst[s]], eng="sync" if s == 0 else "act")
                if s == 0:
                    P.cp(dst[:, :, nb * 512:(nb + 1) * 512], wst[s][:], reads=[r_wst[s]], writes=[r_wd], eng="dve")
                else:
                    P.cp(dst[:, :, nb * 512:(nb + 1) * 512], wst[s][:], reads=[r_wst[s]], writes=[r_wp], eng="pool")
        rW = [r_wd, r_wp]
        ckv_v = ckvT_s.rearrange("(c p) t -> p c t", p=128)
        k = 0
        for ti in range(NTI):
            tsl = slice(ti * TT, (ti + 1) * TT)
            P.ld(ckT[:], ckv_v[:, :, tsl], writes=[r_ckT])
            for c in range(4):
                s = c % 2
                P.act(sq[s][:], ckT[:, c, :], AF.Square, reads=[r_ckT], writes=[r_sq[s]])
                P.mm(ssq[:], ones[:], sq[s][:], c == 0, c == 3, reads=[r_ones, r_sq[s]], writes=[r_ssq])
            P.act(rstd[:], ssq[:], AF.Sqrt, bias=EPS, scale=1.0 / 512, reads=[r_ssq], writes=[r_rstd])
            P.op("dve", lambda e: e.reciprocal(out=rstd[:], in_=rstd[:]), reads=[r_rstd], writes=[r_rstd])
            for c in range(4):
                P.stt(cn[:, c, :], ckT[:, c, :], kvg[:, c:c + 1], rstd[:], ALU.mult, ALU.mult, reads=[r_ckT, r_kvg, r_rstd], writes=[r_cn])
            for h in range(16):
                a = k % 3
                k += 1
                for c in range(4):
                    P.mm(pk[a][:, 0:TT], wukb[:, c, h * 128:(h + 1) * 128], cn[:, c, :], c == 0, c == 3, reads=rW + [r_cn], writes=[r_pk[a]])
                P.act(stb[a][:, 0:TT], pk[a][:, 0:TT], AF.Copy, scale=128.0 ** -0.5, reads=[r_pk[a]], writes=[r_stb[a]])
                P.st(KT_s[h, :, tsl], stb[a][:, 0:TT], reads=[r_stb[a]])
            for tb in range(TT // 128):
                tok = slice(tb * 128, (tb + 1) * 128)
                gtok = slice(ti * TT + tb * 128, ti * TT + (tb + 1) * 128)
                for nb in range(4):
                    a = k % 3
                    k += 1
                    for c in range(4):
                        P.mm(pk[a][:], cn[:, c, tok], wuvb[:, c, nb * 512:(nb + 1) * 512], c == 0, c == 3, reads=rW + [r_cn], writes=[r_pk[a]])
                    P.act(stb[a][:], pk[a][:], AF.Copy, reads=[r_pk[a]], writes=[r_stb[a]])
                    P.st(V_s[gtok, nb * 512:(nb + 1) * 512], stb[a][:], reads=[r_stb[a]], eng="act")
        P.barrier()
        P.emit(last=False)
    with ExitStack() as es:
        sb = lambda n, s, d: es.enter_context(nc.sbuf_tensor(UQ(n), s, d))
        ps = lambda n, s, d: es.enter_context(nc.psum_tensor(UQ(n), s, d))
        RB = sb("RB", [128, 16, 640], BF16); r_RB = R("RB")
        ident_b = sb("ident_b", [128, 128], BF16); r_idb = R("idb")
        with ExitStack() as es2:
            sb2 = lambda n, s, d: es2.enter_context(nc.sbuf_tensor(UQ(n), s, d))
            ps2 = lambda n, s, d: es2.enter_context(nc.psum_tensor(UQ(n), s, d))
            relb = sb2("relb", [32, 16], F32); r_relb = R("relb")
            ohm = sb2("ohm", [32, 768], F32); r_ohm = R("ohm")
            J = sb2("J", [128, 128], F32); r_J = R("J")
            identf = sb2("identf", [128, 128], F32); r_idf = R("idf")
            ptab = [ps2(f"ptab{i}", [16, 384], F32) for i in range(2)]; r_ptab = [R(f"ptab{i}") for i in range(2)]
            tabsb = sb2("tabsb", [16, 768], F32); r_tabsb = R("tabsb")
            Hk = [sb2(f"Hk{i}", [128, 640], F32) for i in range(2)]; r_Hk = [R(f"Hk{i}") for i in range(2)]
            prb = [ps2(f"prb{i}", [128, 320], F32) for i in range(4)]; r_prb = [R(f"prb{i}") for i in range(4)]
            P.ld(relb[:], relb_d, writes=[r_relb]); P.ld(ohm[:], ohm_d, writes=[r_ohm]); P.ld(J[:], J_d, writes=[r_J])
            P.ld(identf[:], ident_d, writes=[r_idf])
            P.cp(ident_b[:], identf[:], reads=[r_idf], writes=[r_idb])
            for i in range(2):
                P.mm(ptab[i][:], relb[:], ohm[:, i * 384:(i + 1) * 384], True, True, reads=[r_relb, r_ohm], writes=[r_ptab[i]])
                P.cp(tabsb[:, i * 384:(i + 1) * 384], ptab[i][:], reads=[r_ptab[i]], writes=[r_tabsb])
            r_tabd = R("tabd")
            P.dma("sync", lambda e: e.dma_start(out=tab_s, in_=tabsb[:]), reads=[r_tabsb], writes=[r_tabd])
            for h in range(16):
                b = h % 2
                P.dma("sync", lambda e, b=b, h=h: e.dma_start(out=Hk[b][:], in_=bass.AP(tab_h, h * 768, [[1, 128], [1, 640]])),
                      reads=[r_tabd], writes=[r_Hk[b]], tok_res=r_Hk[b])
                for i in range(2):
                    a = (2 * h + i) % 4
                    P.mm(prb[a][:], J[:], Hk[b][:, i * 320:(i + 1) * 320], True, True, reads=[r_J, r_Hk[b]], writes=[r_prb[a]])
                    P.cp(RB[:, h, i * 320:(i + 1) * 320], prb[a][:], reads=[r_prb[a]], writes=[r_RB], eng="act" if i else "dve")
            if DBG:
                P.st(dbg_rb, RB[:], reads=[r_RB], final=True)
            P.barrier()
            P.emit(last=False)
        ikT2 = sb("ikT2", [128, T], BF16); r_ik = R("ikT2")
        admneg = sb("admneg", [128, 512], F32); r_adm = R("admneg")
        bufA = sb("bufA", [128, T], F32); r_A = R("bufA")
        bufB = sb("bufB", [128, T], F32); r_Bp = R("bufBp"); r_Bt = R("bufBt")
        bufBb = bufB[:].bitcast(BF16)
        Pm = bufBb[:, 0:T]
        PT = bufBb[:, T:2 * T].rearrange("p (kt q) -> p kt q", q=128)
        mb = sb("mb", [128, T], BF16); r_mb = R("mb")
        KTb = [sb(f"KTb{i}", [128, T], BF16) for i in range(2)]; r_KT = [R(f"KTb{i}") for i in range(2)]
        Vb = [sb(f"Vb{i}", [128, NKT, 128], BF16) for i in range(2)]; r_V = [R(f"Vb{i}") for i in range(2)]
        iqb = sb("iqb", [128, 8, 128], BF16); r_iqb = R("iqb")
        iwb = sb("iwb", [128, 16], F32); r_iwb = R("iwb")
        qcb = sb("qcb", [128, 16, 128], BF16); r_qcb = R("qcb")
        zcb = sb("zcb", [128, 2048], F32); r_zcb = R("zcb")
        ycb = sb("ycb", [128, 2048], BF16); r_ycb = R("ycb")
        rr = [sb(f"rr{i}", [128, 512], F32) for i in range(2)]; r_rr = [R(f"rr{i}") for i in range(2)]
        m8 = sb("m8", [128, 8], F32); r_m8 = R("m8")
        thr = sb("thr", [128, 1], F32); r_thr = R("thr")
        mx = sb("mx", [128, 1], F32); r_mx = R("mx")
        rsum = sb("rsum", [128, 1], F32); r_rsum = R("rsum")
        pi = [ps(f"pi{i}", [128, 512], F32) for i in range(2)]; r_pi = [R(f"pi{i}") for i in range(2)]
        pS = [ps(f"pS{i}", [128, 512], F32) for i in range(2)]; r_pS = [R(f"pS{i}") for i in range(2)]
        ptp = [ps(f"ptp{i}", [128, 4, 128], BF16) for i in range(2)]; r_ptp = [R(f"ptp{i}") for i in range(2)]
        po = ps("po", [128, 128], F32); r_po = R("po")
        P.ld(ikT2[0:64, :], ikT_s, writes=[r_ik]); P.ld(ikT2[64:128, :], ikT_s, writes=[r_ik], eng="act")
        P.ld(admneg[:], adm_d, writes=[r_adm])
        iqv = iqT_s.rearrange("(j p) t -> p j t", p=128); qcv = qcT_s.rearrange("(j p) t -> p j t", p=128)
        ci = 0; cs_ = 0; ct = 0; hc = 0
        NEG = -3.39e38
        for k in range(NQ):
            nk = (k + 1) * 512
            nkt = nk // 128
            osl = slice(k * 128, (k + 1) * 128)
            P.ld(iqb[:], iqv[:, :, osl], writes=[r_iqb]); P.ld(iwb[:], iw_s[osl, :], writes=[r_iwb])
            P.ld(qcb[:], qcv[:, :, osl], writes=[r_qcb], eng="act"); P.ld(zcb[:], zc_s[osl, :], writes=[r_zcb], eng="act")
            for t5 in range(nk // 512):
                ksl = slice(t5 * 512, (t5 + 1) * 512)
                for h in range(16):
                    a = ci % 2; ci += 1
                    hp = (h % 2) * 64
                    P.mm(pi[a][:], iqb[hp:hp + 64, h // 2, :], ikT2[hp:hp + 64, ksl], True, True, reads=[r_iqb, r_ik], writes=[r_pi[a]])
                    P.act(rr[a][:], pi[a][:], AF.Relu, reads=[r_pi[a]], writes=[r_rr[a]])
                    if h == 0:
                        P.ts(bufA[:, ksl], rr[a][:], iwb[:, 0:1], ALU.mult, reads=[r_rr[a], r_iwb], writes=[r_A])
                    else:
                        P.stt(bufA[:, ksl], rr[a][:], iwb[:, h:h + 1], bufA[:, ksl], ALU.mult, ALU.add, reads=[r_rr[a], r_iwb, r_A], writes=[r_A])
            P.tt(bufA[:, nk - 512:nk], bufA[:, nk - 512:nk], admneg[:], ALU.add, reads=[r_A, r_adm], writes=[r_A])
            if DBG:
                P.st(dbg_idx[k, :, 0:nk], bufA[:, 0:nk], reads=[r_A], final=True)
            for rnd in range(32):
                src = bufA if rnd == 0 else bufB
                r_src = [r_A] if rnd == 0 else [r_Bp, r_Bt]
                P.op("dve", lambda e, src=src, nk=nk: e.max(out=m8[:], in_=src[:, 0:nk]), reads=r_src, writes=[r_m8])
                if rnd < 31:
                    P.op("dve", lambda e, src=src, nk=nk: e.match_replace(out=bufB[:, 0:nk], in_to_replace=m8[:], in_values=src[:, 0:nk], imm_value=NEG),
                         reads=r_src + [r_m8], writes=[r_Bp, r_Bt])
            P.ts(thr[:], m8[:, 7:8], -1e37, ALU.max, reads=[r_m8], writes=[r_thr])
            P.ts(bufB[:, 0:nk], bufA[:, 0:nk], thr[:, 0:1], ALU.is_ge, reads=[r_A, r_thr], writes=[r_Bp, r_Bt])
            P.ts(mb[:, 0:nk], bufB[:, 0:nk], BIG, ALU.mult, s2=-BIG, op1=ALU.add, reads=[r_Bp, r_Bt], writes=[r_mb])
            if DBG:
                P.st(dbg_mb[k, :, 0:nk], mb[:, 0:nk], reads=[r_mb], final=True)
                P.st(dbg_thr[k], m8[:], reads=[r_m8], final=True)
            for h in range(16):
                b = hc % 2; hc += 1
                P.ld(KTb[b][:, 0:nk], KT_s[h, :, 0:nk], writes=[r_KT[b]], eng="sync")
                P.ld(Vb[b][:, 0:nkt, :], V_s[0:nk, h * 128:(h + 1) * 128].rearrange("(kt p) d -> p kt d", p=128), writes=[r_V[b]], eng="pool")
                for t5 in range(nk // 512):
                    ksl = slice(t5 * 512, (t5 + 1) * 512)
                    a = cs_ % 2; cs_ += 1
                    P.mm(pS[a][:], qcb[:, h, :], KTb[b][:, ksl], True, True, reads=[r_qcb, r_KT[b]], writes=[r_pS[a]])
                    P.tt(bufA[:, ksl], pS[a][:], mb[:, ksl], ALU.add, reads=[r_pS[a], r_mb], writes=[r_A])
                if k >= 1:
                    P.tt(bufA[:, nk - 640:nk], bufA[:, nk - 640:nk], RB[:, h, :], ALU.add, reads=[r_A, r_RB], writes=[r_A])
                else:
                    P.tt(bufA[:, 0:512], bufA[:, 0:512], RB[:, h, 128:640], ALU.add, reads=[r_A, r_RB], writes=[r_A])
                P.op("dve", lambda e, nk=nk: e.tensor_reduce(out=mx[:], in_=bufA[:, 0:nk], axis=AX.X, op=ALU.max), reads=[r_A], writes=[r_mx])
                P.ts(mx[:], mx[:], -1.0, ALU.mult, reads=[r_mx], writes=[r_mx])
                P.act(Pm[:, 0:nk], bufA[:, 0:nk], AF.Exp, bias=mx[:, 0:1], accum_out=rsum[:, 0:1], reads=[r_A, r_mx], writes=[r_Bp, r_rsum])
                for k4 in range(0, nkt, 4):
                    a = ct % 2; ct += 1
                    for q in range(4):
                        kt = k4 + q
                        P.tr(ptp[a][:, q, :], Pm[:, kt * 128:(kt + 1) * 128], ident_b[:], reads=[r_Bp, r_idb], writes=[r_ptp[a]])
                    P.cp(PT[:, k4:k4 + 4, :], ptp[a][:], reads=[r_ptp[a]], writes=[r_Bt], eng="act" if (k4 // 4) % 2 else "dve")
                for kt in range(nkt):
                    P.mm(po[:], PT[:, kt, :], Vb[b][:, kt, :], kt == 0, kt == nkt - 1, reads=[r_Bt, r_V[b]], writes=[r_po])
                P.op("dve", lambda e: e.reciprocal(out=rsum[:], in_=rsum[:]), reads=[r_rsum], writes=[r_rsum])
                P.stt(ycb[:, h * 128:(h + 1) * 128], po[:], rsum[:, 0:1], zcb[:, h * 128:(h + 1) * 128], ALU.mult, ALU.mult,
                      reads=[r_po, r_rsum, r_zcb], writes=[r_ycb])
            P.st(yc[osl, :], ycb[:], reads=[r_ycb], final=True)
        P.emit(last=True)


def kernel(**inp):
    inp = {k: np.asarray(v) for k, v in inp.items()}
    B, T, D = inp["x"].shape
    cores = list(range(8))
    nc1 = build_ab(T)
    r1 = run_bass_kernel_spmd(nc1, [ab_inputs(inp, c // 4, c % 4, T) for c in cores], core_ids=cores).results
    y = np.zeros((B, T, 4096), dtype=r1[0]["ya"].dtype)
    for c in cores:
        b, g = c // 4, c % 4
        y[b, :, 512 * g:512 * g + 512] = r1[c]["ya"]
        y[b, :, 2048 + 512 * g:2048 + 512 * g + 512] = r1[c]["yb"]
    del r1
    Tt = B * T // 8
    xf = inp["x"].reshape(B * T, D)
    nc2 = build_out(4096, Tt, False)
    r2 = run_bass_kernel_spmd(nc2, [out_inputs(y.reshape(B * T, 4096), inp["ab_w_out"][0], xf, c, Tt) for c in cores],
                              core_ids=cores).results
    x1 = np.concatenate([r["out"] for r in r2], 0).reshape(B, T, D)
    del r2, y
    nc3 = build_cd(T)
    r3 = run_bass_kernel_spmd(nc3, [cd_inputs(inp, x1, c // 4, c % 4, T) for c in cores], core_ids=cores).results
    y2 = np.zeros((B, T, 6144), dtype=r3[0]["yd"].dtype)
    NQ = T // 512
    for c in cores:
        b, g = c // 4, c % 4
        own = np.concatenate([np.arange((4 * k + g) * 128, (4 * k + g) * 128 + 128) for k in range(NQ)])
        y2[b, own, 0:2048] = r3[c]["yc"]
        y2[b, :, 2048 + 1024 * g:2048 + 1024 * g + 1024] = r3[c]["yd"]
    del r3
    nc4 = build_out(6144, Tt, True)
    r4 = run_bass_kernel_spmd(nc4, [out_inputs(y2.reshape(B * T, 6144), inp["cd_w_out"][0], x1.reshape(B * T, D), c, Tt,
                                               gfin=inp["final_norm"]) for c in cores], core_ids=cores).results
    out = np.concatenate([r["out"] for r in r4], 0).reshape(B, T, D)
    return np.ascontiguousarray(out.astype(np.float32))
```
